# Optimizing a Trainium2 kernel written in Bass

```python
import jax, jax.numpy as jnp
from jax import lax
import numpy as np

D_MODEL = 1024
BATCH = 2
SEQ = 8192
DEPTH = 1

EPS = 1e-6
HEAD_DIM = 64
ATTN_WIDTH = D_MODEL // 2
N_Q_HEADS = ATTN_WIDTH // HEAD_DIM
N_KV_HEADS = 2
Q_PER_KV = N_Q_HEADS // N_KV_HEADS
KV_WIDTH = N_KV_HEADS * HEAD_DIM
WINDOW = 128
SSM_INNER = D_MODEL // 2
SSM_HEAD_DIM = 64
SSM_HEADS = SSM_INNER // SSM_HEAD_DIM
SSM_GROUPS = 2
HEADS_PER_GROUP = SSM_HEADS // SSM_GROUPS
D_STATE = 128
BC_WIDTH = SSM_GROUPS * D_STATE
CONV_K = 4
CONV_DIM = SSM_INNER + 2 * BC_WIDTH
CHUNK = 128
MIX_WIDTH = ATTN_WIDTH + SSM_INNER
PROJ_WIDTH = ATTN_WIDTH + 2 * KV_WIDTH + SSM_INNER + CONV_DIM + SSM_HEADS
SPLIT_OFFSETS = [ATTN_WIDTH,
                 ATTN_WIDTH + KV_WIDTH,
                 ATTN_WIDTH + 2 * KV_WIDTH,
                 ATTN_WIDTH + 2 * KV_WIDTH + SSM_INNER,
                 ATTN_WIDTH + 2 * KV_WIDTH + SSM_INNER + CONV_DIM]
PEER_HEADS = 8
D_KEY = 128
HALF_KEY = D_KEY // 2
N_KEYS = 128
N_EXPERTS = N_KEYS * N_KEYS
PEER_TOPK = 16
PEER_BLOCK = 128

kernel_name = 'hymba_swa_ssd_peer_layer'


def rmsnorm(x, g):
    xf = x.astype(jnp.float32)
    y = xf * lax.rsqrt(jnp.mean(xf * xf, axis=-1, keepdims=True) + EPS)
    return (y * g.astype(jnp.float32)).astype(x.dtype)


def alibi_slopes():
    return jnp.exp2(-(8.0 / N_Q_HEADS) * jnp.arange(1, N_Q_HEADS + 1, dtype=jnp.float32))


def sliding_window_attention(q, k, v, sinks):
    b, s = q.shape[0], q.shape[1]
    nb = s // WINDOW
    qb = q.reshape(b, nb, WINDOW, N_KV_HEADS, Q_PER_KV, HEAD_DIM)
    kb = k.reshape(b, nb, WINDOW, N_KV_HEADS, HEAD_DIM)
    vb = v.reshape(b, nb, WINDOW, N_KV_HEADS, HEAD_DIM)
    pad = ((0, 0), (1, 0), (0, 0), (0, 0), (0, 0))
    kk = jnp.concatenate([jnp.pad(kb[:, :-1], pad), kb], axis=2)
    vv = jnp.concatenate([jnp.pad(vb[:, :-1], pad), vb], axis=2)
    logits = jnp.einsum('bnqhgd,bnkhd->bnhgqk', qb, kk).astype(jnp.float32) * (HEAD_DIM ** -0.5)
    qi = jnp.arange(WINDOW)[:, None]
    ki = jnp.arange(2 * WINDOW)[None, :]
    dist = WINDOW + qi - ki
    band = (dist >= 0) & (dist < WINDOW)
    not_first = jnp.arange(nb)[:, None, None] > 0
    valid = band[None] & (not_first | (ki >= WINDOW)[None])
    alibi = -alibi_slopes()[:, None, None] * dist.astype(jnp.float32)[None]
    logits = logits + alibi.reshape(N_KV_HEADS, Q_PER_KV, WINDOW, 2 * WINDOW)[None, None]
    logits = jnp.where(valid[None, :, None, None], logits, -jnp.inf)
    sink = sinks.astype(jnp.float32).reshape(N_KV_HEADS, Q_PER_KV)[None, None, :, :, None, None]
    m = jnp.maximum(jnp.max(logits, axis=-1, keepdims=True), sink)
    p = jnp.exp(logits - m)
    p = p / (jnp.sum(p, axis=-1, keepdims=True) + jnp.exp(sink - m))
    out = jnp.einsum('bnhgqk,bnkhd->bnqhgd', p.astype(vv.dtype), vv)
    return out.reshape(b, s, ATTN_WIDTH)


def ssd_chunked(xh, dt, a_log, bm, cm):
    b, s = xh.shape[0], xh.shape[1]
    nc = s // CHUNK
    a_dt = dt * (-jnp.exp(a_log.astype(jnp.float32)))
    xd = xh.astype(jnp.float32) * dt[..., None]
    X = xd.reshape(b, nc, CHUNK, SSM_GROUPS, HEADS_PER_GROUP, SSM_HEAD_DIM)
    A = a_dt.reshape(b, nc, CHUNK, SSM_GROUPS, HEADS_PER_GROUP).transpose(0, 3, 4, 1, 2)
    Bc = bm.astype(jnp.float32).reshape(b, nc, CHUNK, SSM_GROUPS, D_STATE)
    Cc = cm.astype(jnp.float32).reshape(b, nc, CHUNK, SSM_GROUPS, D_STATE)
    a_cs = jnp.cumsum(A, axis=-1)
    causal = jnp.tril(jnp.ones((CHUNK, CHUNK), dtype=bool))
    l_mat = jnp.exp(jnp.where(causal, a_cs[..., :, None] - a_cs[..., None, :], -jnp.inf))
    cb = jnp.einsum('bclgn,bcsgn->bcgls', Cc, Bc)
    y_diag = jnp.einsum('bcgls,bgrcls,bcsgrp->bclgrp', cb, l_mat, X)
    decay_states = jnp.exp(a_cs[..., -1:] - a_cs)
    states = jnp.einsum('bcsgn,bgrcs,bcsgrp->bcgrpn', Bc, decay_states, X)
    chunk_decay = jnp.exp(a_cs[..., -1])

    def step(state, inp):
        st, dec = inp
        return dec[..., None, None] * state + st, state

    _, prev = lax.scan(step, jnp.zeros_like(states[:, 0]),
                       (jnp.moveaxis(states, 1, 0), jnp.moveaxis(chunk_decay, -1, 0)))
    prev = jnp.moveaxis(prev, 0, 1)
    y_off = jnp.einsum('bclgn,bcgrpn,bgrcl->bclgrp', Cc, prev, jnp.exp(a_cs))
    return (y_diag + y_off).reshape(b, s, SSM_HEADS, SSM_HEAD_DIM)


def token_mixer(xn, w_in, attn_sinks, attn_out_g, conv_w, conv_b, dt_bias, a_log, d_skip, ssm_norm_g, w_out):
    b, s, _ = xn.shape
    proj = xn @ w_in
    q, k, v, z, xbc, dt_raw = jnp.split(proj, SPLIT_OFFSETS, axis=-1)
    attn = sliding_window_attention(q.reshape(b, s, N_Q_HEADS, HEAD_DIM),
                                    k.reshape(b, s, N_KV_HEADS, HEAD_DIM),
                                    v.reshape(b, s, N_KV_HEADS, HEAD_DIM), attn_sinks)
    attn = rmsnorm(attn, attn_out_g)
    xbc = lax.conv_general_dilated(xbc, conv_w.astype(xbc.dtype)[:, None, :], window_strides=(1,),
                                   padding=[(CONV_K - 1, 0)], dimension_numbers=('NWC', 'WIO', 'NWC'),
                                   feature_group_count=CONV_DIM) + conv_b
    xbc = jax.nn.silu(xbc)
    xs, bm, cm = jnp.split(xbc, [SSM_INNER, SSM_INNER + BC_WIDTH], axis=-1)
    dt = jax.nn.softplus(dt_raw.astype(jnp.float32) + dt_bias.astype(jnp.float32))
    xh = xs.reshape(b, s, SSM_HEADS, SSM_HEAD_DIM)
    y = ssd_chunked(xh, dt, a_log, bm.reshape(b, s, SSM_GROUPS, D_STATE), cm.reshape(b, s, SSM_GROUPS, D_STATE))
    y = y + d_skip.astype(jnp.float32)[:, None] * xh.astype(jnp.float32)
    y = y.reshape(b, s, SSM_INNER) * jax.nn.silu(z.astype(jnp.float32))
    yg = y.reshape(b, s, SSM_GROUPS, SSM_INNER // SSM_GROUPS)
    yg = yg * lax.rsqrt(jnp.mean(yg * yg, axis=-1, keepdims=True) + EPS)
    y = (yg.reshape(b, s, SSM_INNER) * ssm_norm_g.astype(jnp.float32)).astype(xn.dtype)
    mix = jnp.concatenate([attn, y], axis=-1)
    return mix @ w_out


def peer_ffn(xn, peer_wq, peer_sub_keys, peer_u, peer_v):
    b, s, d = xn.shape
    t = xn.reshape(b * s, d)
    n_tok = b * s
    q = (t @ peer_wq).reshape(n_tok, PEER_HEADS, 2, HALF_KEY)
    s1 = jnp.einsum('thd,kd->thk', q[:, :, 0], peer_sub_keys[0]).astype(jnp.float32)
    s2 = jnp.einsum('thd,kd->thk', q[:, :, 1], peer_sub_keys[1]).astype(jnp.float32)
    v1, i1 = lax.top_k(s1, PEER_TOPK)
    v2, i2 = lax.top_k(s2, PEER_TOPK)
    cand = (v1[..., :, None] + v2[..., None, :]).reshape(n_tok, PEER_HEADS, PEER_TOPK * PEER_TOPK)
    cand_idx = (i1[..., :, None] * N_KEYS + i2[..., None, :]).reshape(n_tok, PEER_HEADS, PEER_TOPK * PEER_TOPK)
    top_s, pos = lax.top_k(cand, PEER_TOPK)
    idx = jnp.take_along_axis(cand_idx, pos, axis=-1)
    gate = jax.nn.softmax(top_s, axis=-1)
    nblk = n_tok // PEER_BLOCK
    hk = PEER_HEADS * PEER_TOPK
    tb = t.reshape(nblk, PEER_BLOCK, d)
    ib = idx.reshape(nblk, PEER_BLOCK, hk)
    gb = gate.reshape(nblk, PEER_BLOCK, hk).astype(xn.dtype)

    def expert_block(args):
        xb, ibk, gbk = args
        u = jnp.take(peer_u, ibk, axis=0)
        act = gbk * jax.nn.gelu(jnp.einsum('tkd,td->tk', u, xb), approximate=False)
        vv = jnp.take(peer_v, ibk, axis=0)
        return jnp.einsum('tk,tkd->td', act, vv)

    out = lax.map(expert_block, (tb, ib, gb))
    return out.reshape(b, s, d)


def setup_inputs(seed: int = 0) -> dict:
    key = jax.random.key(seed)
    ks = jax.random.split(key, 20)
    L, D = DEPTH, D_MODEL
    f32 = jnp.float32
    nrm = lambda k, shape, sc: jax.random.normal(k, shape, f32) * sc
    dt0 = jnp.exp(jax.random.uniform(ks[7], (L, SSM_HEADS), f32, np.log(1e-3), np.log(1e-1)))
    return {
        'x': nrm(ks[0], (BATCH, SEQ, D), 1.0),
        'norm_mix_g': 1.0 + nrm(ks[1], (L, D), 0.02),
        'w_in': nrm(ks[2], (L, D, PROJ_WIDTH), D ** -0.5),
        'attn_sinks': nrm(ks[3], (L, N_Q_HEADS), 0.5),
        'attn_out_g': 1.0 + nrm(ks[4], (L, ATTN_WIDTH), 0.02),
        'conv_w': nrm(ks[5], (L, CONV_K, CONV_DIM), CONV_K ** -0.5),
        'conv_b': nrm(ks[6], (L, CONV_DIM), 0.02),
        'dt_bias': dt0 + jnp.log(-jnp.expm1(-dt0)),
        'a_log': jnp.log(jax.random.uniform(ks[8], (L, SSM_HEADS), f32, 1.0, 16.0)),
        'd_skip': 1.0 + nrm(ks[9], (L, SSM_HEADS), 0.1),
        'ssm_norm_g': 1.0 + nrm(ks[10], (L, SSM_INNER), 0.02),
        'w_out': nrm(ks[11], (L, MIX_WIDTH, D), MIX_WIDTH ** -0.5),
        'norm_ffn_g': 1.0 + nrm(ks[12], (L, D), 0.02),
        'peer_wq': nrm(ks[13], (L, D, PEER_HEADS * D_KEY), D ** -0.5),
        'peer_sub_keys': nrm(ks[14], (L, 2, N_KEYS, HALF_KEY), HALF_KEY ** -0.5),
        'peer_u': nrm(ks[15], (L, N_EXPERTS, D), D ** -0.5),
        'peer_v': nrm(ks[16], (L, N_EXPERTS, D), PEER_TOPK ** -0.5),
        'final_norm_g': 1.0 + nrm(ks[17], (D,), 0.02),
    }


def reference(x, norm_mix_g, w_in, attn_sinks, attn_out_g, conv_w, conv_b, dt_bias, a_log, d_skip,
              ssm_norm_g, w_out, norm_ffn_g, peer_wq, peer_sub_keys, peer_u, peer_v, final_norm_g):
    h = x
    for l in range(DEPTH):
        xn = rmsnorm(h, norm_mix_g[l])
        h = h + token_mixer(xn, w_in[l], attn_sinks[l], attn_out_g[l], conv_w[l], conv_b[l], dt_bias[l],
                            a_log[l], d_skip[l], ssm_norm_g[l], w_out[l])
        xn = rmsnorm(h, norm_ffn_g[l])
        h = h + peer_ffn(xn, peer_wq[l], peer_sub_keys[l], peer_u[l], peer_v[l])
    return rmsnorm(h, final_norm_g)
```

```python
import numpy as np
import concourse.bass as bass
import concourse.mybir as mybir
from concourse.bass_utils import run_bass_kernel_spmd

F32 = mybir.dt.float32
BF16 = mybir.dt.bfloat16
U8 = mybir.dt.uint8
AF = mybir.ActivationFunctionType
OP = mybir.AluOpType
AX = mybir.AxisListType

EPS = 1e-6
NCORES = 8
NT = 16
NPRE = 48
NEG = 32
WF = 2048
WT = 648
NEGBIG = -240000.0


class Prog:
    ENG = ["pe", "act", "dve", "pool", "sp"]

    def __init__(self):
        self.ops = []
        self.last_w = {}
        self.readers = {}
        self.bar = None
        self.frozen = False
        self.capture = None
        self.last_dma = {}
        self.dcount = {}
        self.limit = 10 ** 9

    def stage(self, n):
        if n > self.limit:
            self.frozen = True

    def _add(self, eng, fn, reads, writes, dma, force=False):
        if self.frozen and not force:
            return -1
        psr = tuple(k for k in reads if k[0] == "ps")
        if psr:
            reads = tuple(k for k in reads if k[0] != "ps")
            writes = tuple(writes) + tuple(k for k in psr if k not in writes)
        idx = len(self.ops)
        deps = set()
        for k in reads:
            if k in self.last_w:
                deps.add(self.last_w[k])
        for k in writes:
            if k in self.last_w:
                deps.add(self.last_w[k])
            for r in self.readers.get(k, ()):
                deps.add(r)
        if self.bar is not None:
            deps.add(self.bar)
        deps.discard(idx)
        self.ops.append(dict(eng=eng, fn=fn, deps=deps, dma=dma, inc=False))
        for k in reads:
            self.readers.setdefault(k, []).append(idx)
        for k in writes:
            self.last_w[k] = idx
            self.readers[k] = []
        return idx

    def op(self, eng, fn, reads=(), writes=(), force=False):
        if self.capture is not None:
            self.capture.append(("op", eng, fn, tuple(reads), tuple(writes), force, None))
            return -2
        return self._add(eng, fn, tuple(reads), tuple(writes), False, force)

    def replay(self, item):
        kind, eng, fn, reads, writes, force, cls = item
        if kind == "op":
            return self._add(eng, fn, reads, writes, False, force)
        return self.dma(eng, fn, reads, writes, force, cls)

    def merged(self, *lists, speed=None):
        if speed is None:
            speed = [1.0] * len(lists)
        keep = [i for i, l in enumerate(lists) if l]
        speed = [speed[i] for i in keep]
        lists = [lists[i] for i in keep]
        pos = [0] * len(lists)
        while True:
            best, bf = -1, 1e9
            for i, l in enumerate(lists):
                if pos[i] < len(l):
                    f = (pos[i] + 0.5) / len(l) * speed[i]
                    if f < bf:
                        best, bf = i, f
            if best < 0:
                break
            self.replay(lists[best][pos[best]])
            pos[best] += 1

    def dma(self, eng, fn, reads=(), writes=(), force=False, cls="c0"):
        if self.capture is not None:
            self.capture.append(("dma", eng, fn, tuple(reads), tuple(writes), force, cls))
            return -2
        idx = self._add(eng, fn, tuple(reads), tuple(writes), True, force)
        if idx < 0:
            return idx
        o = self.ops[idx]
        if cls in self.last_dma:
            o["deps"].add(self.last_dma[cls])
        self.last_dma[cls] = idx
        self.dcount[cls] = self.dcount.get(cls, 0) + 1
        o["cls"] = cls
        o["val"] = ("d", cls, 16 * self.dcount[cls])
        return idx

    def barrier(self):
        keys = tuple(set(self.last_w.keys()) | set(self.readers.keys()))
        last = None
        for e in self.ENG:
            last = self._add(e, None, (), keys, False)
        if last is not None and last >= 0:
            self.bar = last

    def emit(self, nc, sems, dsems, block):
        ops = self.ops
        for o in ops:
            for d in o["deps"]:
                ops[d]["inc"] = True
        cnt = {e: 0 for e in self.ENG}
        dcnt = {e: 0 for e in self.ENG}
        for o in ops:
            e = o["eng"]
            if o["dma"]:
                pass
            else:
                if o["fn"] is None and o["inc"]:
                    pass
                if o["inc"]:
                    cnt[e] += 1
                o["val"] = ("c", e, cnt[e])
        per = {e: [] for e in self.ENG}
        for i, o in enumerate(ops):
            per[o["eng"]].append(i)

        def run(eng_name, eng):
            known = {}
            for i in per[eng_name]:
                o = ops[i]
                need = {}
                for d in o["deps"]:
                    kind, e, v = ops[d]["val"]
                    if kind == "c" and not ops[d]["inc"]:
                        continue
                    key = (kind, e)
                    if v > need.get(key, 0):
                        need[key] = v
                for key, v in need.items():
                    if known.get(key, 0) >= v:
                        continue
                    known[key] = v
                    s = dsems[key[1]] if key[0] == "d" else sems[key[1]]
                    eng.wait_ge(s, v)
                if o["fn"] is None:
                    if o["inc"]:
                        eng.nop().then_inc(sems[eng_name], 1)
                    continue
                ins = o["fn"](eng)
                if o["dma"]:
                    ins.then_inc(dsems[o["cls"]], 16)
                elif o["inc"]:
                    ins.then_inc(sems[eng_name], 1)

        @block.tensor
        def _(e):
            run("pe", e)

        @block.scalar
        def _(e):
            run("act", e)

        @block.vector
        def _(e):
            run("dve", e)

        @block.gpsimd
        def _(e):
            run("pool", e)

        @block.sync
        def _(e):
            run("sp", e)


def build(nt=NT, npre=NPRE, neg=NEG, dbg=False, limit=10 ** 9):
    nc = bass.Bass("TRN2", target_bir_lowering=False)
    P = Prog()
    P.limit = limit
    ntok = nt * 128

    def din(name, shape, dt=F32):
        return nc.dram_tensor(name, list(shape), dt, kind="ExternalInput").ap()

    x_own = din("x_own", [ntok, 1024])
    x_pre = din("x_pre", [max(npre, 1) * 128, 1024])
    pflag_d = din("pflag", [128, max(npre, 1)])
    w_d = din("w_in", [128, 8, WF + WT])
    wo_d = din("w_out", [128, 8, 1024])
    wq_d = din("wq", [128, 8, 1024])
    k12_d = din("k12", [128, 2, 128])
    ut_d = din("ut", [neg, 128, 8, 512])
    kk_d = din("kk", [neg, 128, 512])
    v_d = din("vv", [neg, 128, 4, 1024])
    cvec_d = din("cvec", [128, 32])
    cpart_d = din("cpart", [128, 72])
    fng_d = din("fng", [128, 1024])
    cst_d = din("cst", [128, 4, 128])
    bias_d = din("abias", [3, 128, 2, 512])
    out_d = nc.dram_tensor("out", [ntok, 1024], F32, kind="ExternalOutput").ap()
    if dbg:
        dbg_h = nc.dram_tensor("dbg_h", [ntok, 1024], F32, kind="ExternalOutput").ap()
        dbg_mix = nc.dram_tensor("dbg_mix", [ntok, 1024], BF16, kind="ExternalOutput").ap()

    ctx = []

    def enter(cm):
        ctx.append(cm)
        return cm.__enter__()

    ARENA = 204 * 1024
    arena = enter(nc.sbuf_tensor("arena", [128, ARENA], U8))
    psum = enter(nc.psum_tensor("psum", [128, 4096], F32))
    off = [0]

    def alloc(shape, dt, at=None):
        n = 1
        for s in shape:
            n *= s
        nb = n * (4 if dt == F32 else 2)
        nb_al = (nb + 31) // 32 * 32
        if at is None:
            o = off[0]
            off[0] += nb_al
        else:
            o = at
        assert o + nb_al <= ARENA, (o, nb_al)
        v = arena[:, o:o + nb].bitcast(dt)
        if len(shape) == 2:
            v = v.rearrange("p (a b) -> p a b", a=shape[0])
        elif len(shape) == 3:
            v = v.rearrange("p (a b c) -> p a b c", a=shape[0], b=shape[1])
        return v

    def pbank(b, dt=F32):
        v = psum[:, b * 512:(b + 1) * 512]
        if dt != F32:
            v = v.bitcast(dt)
        return v

    def PS(b):
        return ("ps", b)

    h = alloc([nt, 1024], F32)
    cst = alloc([4, 128], F32)
    ident_f, tri_le, tri_gt, ones_f = cst[:, 0, :], cst[:, 1, :], cst[:, 2, :], cst[:, 3, :]
    ident_b = alloc([128], BF16)
    cvec = alloc([32], F32)
    cpart = alloc([72], F32)
    cder = alloc([32], F32)
    small = alloc([64], F32)
    persist_end = off[0]

    g_mix, g_ffn, gcat = cpart[:, 0:8], cpart[:, 8:16], cpart[:, 16:24]
    convw = cpart[:, 24:56].rearrange("p (k c) -> p k c", k=4)
    convb = cpart[:, 56:64]
    sinks_b, dtb_b, alog_b, dskip_b = cvec[:, 0:8], cvec[:, 8:16], cvec[:, 16:24], cvec[:, 24:32]
    esink, negA = cder[:, 0:8], cder[:, 8:16]

    w_sb = alloc([8, WF + WT], BF16)
    wo_sb = alloc([8, 1024], BF16)
    abias = alloc([3, 2, 512], BF16)
    pflag = alloc([max(npre, 1)], F32)
    xtmp = [alloc([1024], F32) for _ in range(2)]
    junk = alloc([1024], BF16)
    xs_bf = alloc([1024], BF16)
    xnT = alloc([8, 128], BF16)
    qT = alloc([4, 128], BF16)
    kT = [alloc([2, 2, 128], BF16) for _ in range(2)]
    vext = [alloc([2, 128], BF16) for _ in range(2)]
    xpre = alloc([8, 131], F32)
    cacc = alloc([8, 128], F32)
    xcTs = [alloc([6, 128], F32) for _ in range(2)]
    bcT = alloc([4, 128], BF16)
    zs = alloc([512], F32)
    dtss = [alloc([64], F32) for _ in range(2)]
    dec = alloc([3, 8], F32)
    xsk = alloc([512], F32)
    xdt = alloc([512], BF16)
    xdec = alloc([512], BF16)
    btok = alloc([2, 128], BF16)
    state = alloc([512], F32)
    stateb = alloc([512], BF16)
    stmp = alloc([512], F32)
    rhsA = alloc([8, 128], F32)
    cbm = alloc([2, 128], F32)
    cblt = alloc([8, 128], BF16)
    t1 = alloc([512], F32)
    t2 = alloc([512], F32)
    pT_sb = [alloc([512], BF16) for _ in range(4)]
    attn = alloc([512], F32)
    mix = alloc([1024], BF16)
    mixT = alloc([8, 128], BF16)
    ytok = [alloc([512], F32) for _ in range(2)]
    phase1_end = off[0]

    K = lambda name: ("sb", name)

    if limit < 10 ** 9:
        for ti in range(nt):
            P.op("pool", lambda e, ti=ti: e.memset(h[:, ti, :], 0.0), writes=[K("h%d" % ti)], force=True)
    P.stage(0.05)
    P.dma("sp", lambda e: e.dma_start(out=cst, in_=cst_d), writes=[K("cst")])
    P.dma("sp", lambda e: e.dma_start(out=cvec, in_=cvec_d), writes=[K("cvec")])
    P.dma("sp", lambda e: e.dma_start(out=cpart, in_=cpart_d), writes=[K("cpart")])
    P.dma("sp", lambda e: e.dma_start(out=pflag, in_=pflag_d), writes=[K("pflag")])
    P.stage(0.1)
    P.dma("pool", lambda e: e.dma_start(out=w_sb.rearrange("p k (a b) -> p (k a) b", b=337),
                                        in_=w_d.rearrange("p k (a b) -> p (k a) b", b=337)),
          writes=[K("w_sb%d" % k) for k in range(8)], cls="c1")
    P.stage(0.2)
    P.dma("pool", lambda e: e.dma_start(out=wo_sb, in_=wo_d), writes=[K("wo_sb%d" % k) for k in range(8)], cls="c2")
    P.stage(0.3)
    for j in range(3):
        P.dma("pool", lambda e, j=j: e.dma_start(out=abias[:, j, :, :], in_=bias_d[j]),
              writes=[K("abias")] if j == 2 else [K("abias%d" % j)], cls="c3")
    P.stage(0.4)
    P.op("dve", lambda e: e.tensor_copy(out=ident_b, in_=ident_f), reads=[K("cst")], writes=[K("identb")])
    for k in range(8):
        P.op("pool", lambda e, k=k: e.tensor_scalar(out=wo_sb[:, k, :], in0=wo_sb[:, k, :], scalar1=gcat[:, k:k + 1],
                                                    scalar2=None, op0=OP.mult),
             reads=[K("cpart"), K("wo_sb%d" % k)], writes=[K("wo_sb%d" % k)])
    WKEYS = [K("w_sb%d" % k) for k in range(8)]
    WOKEYS = [K("wo_sb%d" % k) for k in range(8)]
    P.stage(0.5)
    P.op("act", lambda e: e.activation(out=esink, in_=sinks_b, func=AF.Exp), reads=[K("cvec")], writes=[K("esink")])
    P.op("act", lambda e: e.activation(out=negA, in_=alog_b, func=AF.Exp), reads=[K("cvec")], writes=[K("negA")])
    P.op("dve", lambda e: e.tensor_scalar(out=negA, in0=negA, scalar1=-1.0, scalar2=None, op0=OP.mult),
         reads=[K("negA")], writes=[K("negA")])
    P.stage(0.6)
    P.op("dve", lambda e: e.memset(state, 0.0), writes=[K("state")])
    P.op("dve", lambda e: e.memset(stateb, 0.0), writes=[K("stateb")])
    P.op("dve", lambda e: e.memset(xpre, 0.0), writes=[K("xpre")])
    for par in range(2):
        P.op("pool", lambda e, par=par: e.memset(vext[par], 1.0), writes=[K("vext%d" % par)])
        P.op("pool", lambda e, par=par: e.memset(kT[par], 0.0), writes=[K("kT%d" % par)])

    P.stage(1)
    def norm_transpose(src, srckey, gcol, dst, dstkey, tag):
        ss = small[:, 0:1]
        rs = small[:, 1:2]
        P.op("act", lambda e: e.activation(out=junk, in_=src, func=AF.Square, accum_out=ss),
             reads=[srckey], writes=[K("junk"), K("ss")])
        P.op("act", lambda e: e.activation(out=rs, in_=ss, func=AF.Sqrt, scale=1.0 / 1024, bias=epsc),
             reads=[K("ss"), K("epsc")], writes=[K("rs")])
        P.op("dve", lambda e: e.reciprocal(out=rs, in_=rs), reads=[K("rs")], writes=[K("rs")])
        P.op("dve", lambda e: e.tensor_scalar(out=xs_bf, in0=src, scalar1=rs, scalar2=None, op0=OP.mult),
             reads=[srckey, K("rs")], writes=[K("xs_bf")])
        pt = pbank(0, BF16)
        for k in range(8):
            P.op("pe", lambda e, k=k: e.transpose(out=pt[:, k * 128:(k + 1) * 128], in_=xs_bf[:, k * 128:(k + 1) * 128],
                                                  identity=ident_b),
                 reads=[K("xs_bf"), K("identb")], writes=[PS(0)])
        P.op("dve", lambda e: e.tensor_tensor(out=dst, in0=pt.rearrange("p (k t) -> p k t", k=8),
                                              in1=gcol.unsqueeze(2).broadcast_to([128, 8, 128]), op=OP.mult),
             reads=[PS(0), K("cpart")], writes=[dstkey])

    epsc = small[:, 2:3]
    P.op("dve", lambda e: e.memset(epsc, EPS), writes=[K("epsc")])

    ytc = [0]

    def mixer_tile(slot, phase="both"):
        own = slot >= npre
        last_pre = (slot == npre - 1)
        pipe = (not own) and (not last_pre)
        ti = slot - npre
        par = slot % 2
        ppar = 1 - par
        xcT = xcTs[par]
        dts = dtss[par]
        if own:
            src = h[:, ti, :]
            srckey = K("h%d" % ti)
        else:
            src = xtmp[par]
            srckey = K("xtmp%d" % par)
        if own:
            fch = list(range(16))
        elif last_pre:
            fch = list(range(4, 16))
        else:
            fch = list(range(8, 14))
        vlo = WF + 512 if (own or last_pre) else WF + 640
        nv = WF + WT - vlo
        if pipe:
            bxs, bB = (3, 4) if par == 0 else (1, 2)
            bdt, dcol = bB, 256
            bTX, bTB, bct = 5, 6, 7
        else:
            bxs, bB = 3, 4
            bdt, dcol = 6, nv - 8
            bTX, bTB, bct = (5, 6, 6) if own else (1, 2, 4)

        def fdst(cid):
            if pipe:
                return (bxs, cid - 8) if cid < 12 else (bB, cid - 12)
            return 1 + cid // 4, cid % 4

        if phase != "back":
            if own:
                P.dma("sp", lambda e: e.dma_start(out=src, in_=x_own[ti * 128:(ti + 1) * 128, :]), writes=[srckey], cls="xh%d" % (ti % 2))
            else:
                P.dma("sp", lambda e: e.dma_start(out=src, in_=x_pre[slot * 128:(slot + 1) * 128, :]), writes=[srckey], cls="xp%d" % par)
            norm_transpose(src, srckey, g_mix, xnT, K("xnT"), "m")
            groups = {}
            for cid in fch:
                b, j = fdst(cid)
                groups.setdefault(b, []).append((j, cid))
            glist = []
            for b in sorted(groups):
                lst = sorted(groups[b])
                glist.append((b, lst[0][0], lst[0][1], len(lst)))

            def tback(i):
                b, j0, c0, nch = glist[i]
                yb = ytok[(ytc[0] + i) % 2]
                yk = K("ytok%d" % ((ytc[0] + i) % 2))
                for jj in range(nch):
                    P.op("pe", lambda e, b=b, j0=j0, jj=jj, yb=yb: e.transpose(
                        out=pbank(b)[:, (j0 + jj) * 128:(j0 + jj + 1) * 128], in_=yb[:, jj * 128:(jj + 1) * 128],
                        identity=ident_f),
                        reads=[yk, K("cst")], writes=[PS(b)])

            for i, (b, j0, c0, nch) in enumerate(glist):
                for k in range(8):
                    P.op("pe", lambda e, b=b, j0=j0, c0=c0, nch=nch, k=k: e.matmul(
                        out=pbank(b)[:, j0 * 128:(j0 + nch) * 128], lhsT=xnT[:, k, :],
                        rhs=w_sb[:, k, c0 * 128:(c0 + nch) * 128], start=(k == 0), stop=(k == 7)),
                        reads=[K("xnT"), WKEYS[k]], writes=[PS(b)])
                yb = ytok[(ytc[0] + i) % 2]
                yk = K("ytok%d" % ((ytc[0] + i) % 2))
                eng = "act" if (ytc[0] + i) % 2 == 0 else "dve"
                if eng == "act":
                    P.op("act", lambda e, b=b, j0=j0, nch=nch, yb=yb: e.copy(
                        out=yb[:, 0:nch * 128], in_=pbank(b)[:, j0 * 128:(j0 + nch) * 128]),
                        reads=[PS(b)], writes=[yk])
                else:
                    P.op("dve", lambda e, b=b, j0=j0, nch=nch, yb=yb: e.tensor_copy(
                        out=yb[:, 0:nch * 128], in_=pbank(b)[:, j0 * 128:(j0 + nch) * 128]),
                        reads=[PS(b)], writes=[yk])
                if i >= 1:
                    tback(i - 1)
            if own:
                for k in range(8):
                    P.op("pe", lambda e, k=k: e.matmul(out=pbank(5), lhsT=xnT[:, k, :], rhs=w_sb[:, k, WF:WF + 512],
                                                       start=(k == 0), stop=(k == 7)),
                         reads=[K("xnT"), WKEYS[k]], writes=[PS(5)])
            for k in range(8):
                P.op("pe", lambda e, k=k: e.matmul(out=pbank(bdt)[:, dcol + 8 - nv:dcol + 8], lhsT=xnT[:, k, :],
                                                   rhs=w_sb[:, k, vlo:WF + WT], start=(k == 0), stop=(k == 7)),
                     reads=[K("xnT"), WKEYS[k]], writes=[PS(bdt)])
            tback(len(glist) - 1)
            ytc[0] += len(glist)
        if phase == "front":
            return
        P.stage(10 + slot + 0.1)
        if own:
            P.op("act", lambda e: e.copy(out=qT, in_=pbank(1).rearrange("p (c t) -> p c t", c=4)),
                 reads=[PS(1)], writes=[K("qT")])
            P.op("act", lambda e: e.activation(out=zs, in_=pbank(5), func=AF.Silu), reads=[PS(5)], writes=[K("zs")])
        if own or last_pre:
            P.op("act", lambda e: e.copy(out=kT[par], in_=pbank(2).rearrange("p (c f t) -> p c f t", c=2, f=2)),
                 reads=[PS(2)], writes=[K("kT%d" % par)])
            P.op("act", lambda e: e.copy(out=vext[par][:, :, 0:64],
                                         in_=pbank(6)[:, 0:128].rearrange("p (c t) -> p c t", c=2)),
                 reads=[PS(6)], writes=[K("vext%d" % par)])
        dtx, dtv, adt, dtf = dts[:, 0:8], dts[:, 8:16], dts[:, 16:24], dts[:, 24:32]
        P.op("dve", lambda e: e.tensor_tensor(out=dtx, in0=pbank(bdt)[:, dcol:dcol + 8], in1=dtb_b, op=OP.add),
             reads=[PS(bdt), K("cvec")], writes=[K("dtx%d" % par)])
        P.op("act", lambda e: e.activation(out=dtx, in_=dtx, func=AF.Exp), reads=[K("dtx%d" % par)], writes=[K("dtx%d" % par)])
        P.op("act", lambda e: e.activation(out=dtv, in_=dtx, func=AF.Ln, bias=1.0), reads=[K("dtx%d" % par)], writes=[K("dtv%d" % par)])
        P.op("dve", lambda e: e.tensor_tensor(out=adt, in0=dtv, in1=negA, op=OP.mult),
             reads=[K("dtv%d" % par), K("negA")], writes=[K("adt%d" % par)])
        if own:
            dtf_ap, dtfkey = dtv, K("dtv%d" % par)
        else:
            P.op("dve", lambda e: e.tensor_scalar(out=dtf, in0=dtv, scalar1=pflag[:, slot:slot + 1], scalar2=None,
                                                  op0=OP.mult),
                 reads=[K("dtv%d" % par), K("pflag")], writes=[K("dtf%d" % par)])
            dtf_ap, dtfkey = dtf, K("dtf%d" % par)
        P.stage(10 + slot + 0.2)
        cch = list(range(8)) if own else list(range(6))
        ncc = len(cch)
        P.op("act", lambda e: e.copy(out=xpre[:, 0:4, 3:131], in_=pbank(bxs).rearrange("p (c t) -> p c t", c=4)),
             reads=[PS(bxs)], writes=[K("xpre")])
        nb4 = 4 if (own or last_pre) else 2
        P.op("act", lambda e: e.copy(out=xpre[:, 4:4 + nb4, 3:131],
                                     in_=pbank(bB)[:, 0:nb4 * 128].rearrange("p (c t) -> p c t", c=nb4)),
             reads=[PS(bB)], writes=[K("xpre")])
        cap_ssd, cap_att = [], []
        if own:
            P.capture = cap_ssd
        for c in cch:
            P.op("dve", lambda e, c=c: e.tensor_scalar(out=cacc[:, c, :], in0=xpre[:, c, 0:128], scalar1=convw[:, 0, c:c + 1],
                                                       scalar2=convb[:, c:c + 1], op0=OP.mult, op1=OP.add),
                 reads=[K("xpre"), K("cpart")], writes=[K("cacc%d" % c)])
        for kk in range(1, 4):
            for c in cch:
                P.op("dve", lambda e, c=c, kk=kk: e.scalar_tensor_tensor(
                    out=cacc[:, c, :], in0=xpre[:, c, kk:kk + 128], scalar=convw[:, kk, c:c + 1], in1=cacc[:, c, :],
                    op0=OP.mult, op1=OP.add),
                    reads=[K("xpre"), K("cpart"), K("cacc%d" % c)], writes=[K("cacc%d" % c)])
        P.op("pool", lambda e: e.tensor_copy(out=xpre[:, :, 0:3], in_=xpre[:, :, 128:131]),
             reads=[K("xpre")] + [K("cacc%d" % c) for c in cch], writes=[K("xpre")])
        CK = [K("cacc%d" % c) for c in cch]
        P.op("act", lambda e: e.activation(out=xcT, in_=cacc[:, 0:6, :], func=AF.Silu), reads=CK, writes=[K("xcT%d" % par)])
        if own:
            P.op("act", lambda e: e.activation(out=bcT, in_=cacc[:, 4:8, :], func=AF.Silu), reads=CK, writes=[K("bcT")])
        P.stage(10 + slot + 0.3)
        if P.capture is not None and pipe:
            P.capture.append(("marker",))
        for j in range(6):
            b, jj = (bTX, j) if j < 4 else (bTB, j - 4)
            P.op("pe", lambda e, j=j, b=b, jj=jj: e.transpose(out=pbank(b)[:, jj * 128:(jj + 1) * 128], in_=xcT[:, j, :],
                                                              identity=ident_f),
                 reads=[K("xcT%d" % par), K("cst")], writes=[PS(b)])
        for j, lt in enumerate([tri_le, tri_gt, ones_f]):
            P.op("pe", lambda e, j=j, lt=lt: e.matmul(out=pbank(6)[:, 256 + j * 8:256 + (j + 1) * 8], lhsT=lt, rhs=adt,
                                                      start=True, stop=True),
                 reads=[K("adt%d" % par), K("cst")], writes=[PS(6)])
        P.op("act", lambda e: e.activation(out=dec, in_=pbank(6)[:, 256:280].rearrange("p (a b) -> p a b", a=3),
                                           func=AF.Exp),
             reads=[PS(6)], writes=[K("dec")])
        eacs, edst, cdec = dec[:, 0, :], dec[:, 1, :], dec[:, 2, :]
        x3 = pbank(bTX).rearrange("p (h q) -> p h q", h=8)
        P.op("dve", lambda e: e.tensor_tensor(out=xdt.rearrange("p (h q) -> p h q", h=8), in0=x3,
                                              in1=dtf_ap.unsqueeze(2).broadcast_to([128, 8, 64]), op=OP.mult),
             reads=[PS(bTX), dtfkey], writes=[K("xdt")])
        if own:
            P.op("dve", lambda e: e.tensor_tensor(out=xsk.rearrange("p (h q) -> p h q", h=8), in0=x3,
                                                  in1=dskip_b.unsqueeze(2).broadcast_to([128, 8, 64]), op=OP.mult),
                 reads=[PS(bTX), K("cvec")], writes=[K("xsk")])
        P.op("act", lambda e: e.copy(out=btok, in_=pbank(bTB)[:, 0:256].rearrange("p (c t) -> p c t", c=2)),
             reads=[PS(bTB)], writes=[K("btok")])
        P.op("dve", lambda e: e.tensor_tensor(out=xdec.rearrange("p (h q) -> p h q", h=8),
                                              in0=xdt.rearrange("p (h q) -> p h q", h=8),
                                              in1=edst.unsqueeze(2).broadcast_to([128, 8, 64]), op=OP.mult),
             reads=[K("xdt"), K("dec")], writes=[K("xdec")])

        if own:
            P.stage(10 + slot + 0.4)
            for g in range(2):
                P.op("pe", lambda e, g=g: e.matmul(out=pbank(0)[:, g * 128:(g + 1) * 128], lhsT=bcT[:, g, :],
                                                   rhs=bcT[:, 2 + g, :], start=True, stop=True),
                     reads=[K("bcT")], writes=[PS(0)])
            P.op("dve", lambda e: e.tensor_tensor(out=cbm, in0=pbank(0)[:, 0:256].rearrange("p (g l) -> p g l", g=2),
                                                  in1=tri_le.unsqueeze(1).broadcast_to([128, 2, 128]), op=OP.mult),
                 reads=[PS(0), K("cst")], writes=[K("cbm")])
            for hh in range(8):
                P.op("dve", lambda e, hh=hh: e.tensor_scalar(out=rhsA[:, hh, :], in0=tri_le, scalar1=adt[:, hh:hh + 1],
                                                              scalar2=None, op0=OP.mult),
                     reads=[K("cst"), K("adt%d" % par)], writes=[K("rhsA")])
            for half in range(2):
                b = 7 if half == 0 else 5
                P.op("pe", lambda e, half=half, b=b: e.matmul(
                    out=pbank(b), lhsT=tri_gt, rhs=rhsA[:, half * 4:(half + 1) * 4, :], start=True, stop=True),
                    reads=[K("rhsA"), K("cst")], writes=[PS(b)])
                P.op("act", lambda e, b=b: e.activation(out=pbank(b), in_=pbank(b), func=AF.Exp),
                     reads=[PS(b)], writes=[PS(b)])
                P.op("dve", lambda e, half=half, b=b: e.tensor_tensor(
                    out=cblt[:, half * 4:(half + 1) * 4, :], in0=pbank(b).rearrange("p (r l) -> p r l", r=4),
                    in1=cbm[:, half, :].unsqueeze(1).broadcast_to([128, 4, 128]), op=OP.mult),
                    reads=[PS(b), K("cbm")], writes=[K("cblt%d" % half)])
            P.stage(10 + slot + 0.45)
            for hh in range(8):
                P.op("pe", lambda e, hh=hh: e.matmul(out=pbank(7)[:, hh * 64:(hh + 1) * 64], lhsT=cblt[:, hh, :],
                                                     rhs=xdt[:, hh * 64:(hh + 1) * 64], start=True, stop=True),
                     reads=[K("cblt0"), K("cblt1"), K("xdt")], writes=[PS(7)])
            for g in range(2):
                P.op("pe", lambda e, g=g: e.matmul(out=pbank(5)[:, g * 256:(g + 1) * 256], lhsT=bcT[:, 2 + g, :],
                                                   rhs=stateb[:, g * 256:(g + 1) * 256], start=True, stop=True),
                     reads=[K("bcT"), K("stateb")], writes=[PS(5)])
            P.op("dve", lambda e: e.tensor_tensor(out=t1.rearrange("p (h q) -> p h q", h=8),
                                                  in0=pbank(5).rearrange("p (h q) -> p h q", h=8),
                                                  in1=eacs.unsqueeze(2).broadcast_to([128, 8, 64]), op=OP.mult),
                 reads=[PS(5), K("dec")], writes=[K("t1")])
            P.op("dve", lambda e: e.tensor_tensor(out=t2, in0=pbank(7), in1=t1, op=OP.add),
                 reads=[PS(7), K("t1")], writes=[K("t2")])
            P.op("dve", lambda e: e.tensor_tensor(out=t2, in0=t2, in1=xsk, op=OP.add),
                 reads=[K("t2"), K("xsk")], writes=[K("t2")])
            P.op("dve", lambda e: e.tensor_tensor(out=t2, in0=t2, in1=zs, op=OP.mult),
                 reads=[K("t2"), K("zs")], writes=[K("t2")])
            ss2, rs2 = small[:, 4:6], small[:, 6:8]
            for g in range(2):
                P.op("act", lambda e, g=g: e.activation(out=junk[:, 0:256], in_=t2[:, g * 256:(g + 1) * 256],
                                                        func=AF.Square, accum_out=ss2[:, g:g + 1]),
                     reads=[K("t2")], writes=[K("junk"), K("ss2")])
            P.op("act", lambda e: e.activation(out=rs2, in_=ss2, func=AF.Sqrt, scale=1.0 / 256, bias=epsc),
                 reads=[K("ss2"), K("epsc")], writes=[K("rs2")])
            P.op("dve", lambda e: e.reciprocal(out=rs2, in_=rs2), reads=[K("rs2")], writes=[K("rs2")])
            for g in range(2):
                P.op("dve", lambda e, g=g: e.tensor_scalar(out=mix[:, 512 + g * 256:512 + (g + 1) * 256],
                                                           in0=t2[:, g * 256:(g + 1) * 256], scalar1=rs2[:, g:g + 1],
                                                           scalar2=None, op0=OP.mult),
                     reads=[K("t2"), K("rs2")], writes=[K("mixs")])
        P.stage(10 + slot + 0.5)
        for g in range(2):
            P.op("pe", lambda e, g=g: e.matmul(out=pbank(bct)[:, g * 256:(g + 1) * 256], lhsT=btok[:, g, :],
                                               rhs=xdec[:, g * 256:(g + 1) * 256], start=True, stop=True),
                 reads=[K("btok"), K("xdec")], writes=[PS(bct)])
        P.op("dve", lambda e: e.tensor_tensor(out=stmp.rearrange("p (h q) -> p h q", h=8),
                                               in0=state.rearrange("p (h q) -> p h q", h=8),
                                               in1=cdec.unsqueeze(2).broadcast_to([128, 8, 64]), op=OP.mult),
             reads=[K("state"), K("dec")], writes=[K("stmp")])
        P.op("dve", lambda e: e.tensor_tensor(out=state, in0=pbank(bct), in1=stmp, op=OP.add),
             reads=[PS(bct), K("stmp")], writes=[K("state")])
        P.op("act", lambda e: e.copy(out=stateb, in_=state), reads=[K("state")], writes=[K("stateb")])

        if own:
            P.capture = cap_att
            bsel = 0 if ti == 0 else 1
            pi = 0
            for kv in range(2):
                for kb in range(2):
                    b = 1 + (pi % 2)
                    kpar = ppar if kb == 0 else par
                    bj = bsel if kb == 0 else 2
                    P.op("pe", lambda e, b=b, bj=bj, kv=kv: e.matmul(out=pbank(b), lhsT=ident_b, rhs=abias[:, bj, kv, :],
                                                                     start=True, stop=False),
                         reads=[K("identb"), K("abias"), K("abias0"), K("abias1")], writes=[PS(b)])
                    for g in range(4):
                        hq = kv * 4 + g
                        c, hf = hq // 2, hq % 2
                        P.op("pe", lambda e, b=b, g=g, c=c, hf=hf, kv=kv, kpar=kpar: e.matmul(
                            out=pbank(b)[:, g * 128:(g + 1) * 128], lhsT=kT[kpar][:, kv, hf, :],
                            rhs=qT[:, c, :], start=False, stop=(g == 3)),
                            reads=[K("kT%d" % kpar), K("qT")], writes=[PS(b)])
                    P.op("act", lambda e, b=b, pi=pi: e.activation(out=pT_sb[pi], in_=pbank(b), func=AF.Exp, scale=0.125),
                         reads=[PS(b)], writes=[K("pT%d" % pi)])
                    pi += 1
            P.stage(10 + slot + 0.7)
            pav = [pbank(3)[:, 0:260].rearrange("p (h q) -> p h q", h=4), pbank(4)[:, 0:260].rearrange("p (h q) -> p h q", h=4)]
            for hq in range(8):
                kv, g = hq // 4, hq % 4
                for kb in range(2):
                    kpar = ppar if kb == 0 else par
                    P.op("pe", lambda e, hq=hq, kv=kv, g=g, kb=kb, kpar=kpar: e.matmul(
                        out=pav[kv][:, g, :], lhsT=pT_sb[kv * 2 + kb][:, g * 128:(g + 1) * 128], rhs=vext[kpar][:, kv, 0:65],
                        start=(kb == 0), stop=(kb == 1)),
                        reads=[K("pT%d" % (kv * 2 + kb)), K("vext%d" % kpar)], writes=[PS(3 + kv)])
            den, rden = dts[:, 32:40], dts[:, 40:48]
            for kv in range(2):
                P.op("dve", lambda e, kv=kv: e.tensor_tensor(out=den[:, kv * 4:(kv + 1) * 4], in0=pav[kv][:, :, 64],
                                                             in1=esink[:, kv * 4:(kv + 1) * 4], op=OP.add),
                     reads=[PS(3 + kv), K("esink")], writes=[K("den%d" % kv)])
            P.op("dve", lambda e: e.reciprocal(out=rden, in_=den), reads=[K("den0"), K("den1")], writes=[K("rden")])
            for kv in range(2):
                P.op("dve", lambda e, kv=kv: e.tensor_tensor(
                    out=attn[:, kv * 256:(kv + 1) * 256].rearrange("p (h q) -> p h q", h=4), in0=pav[kv][:, :, 0:64],
                    in1=rden[:, kv * 4:(kv + 1) * 4].unsqueeze(2).broadcast_to([128, 4, 64]), op=OP.mult),
                    reads=[PS(3 + kv), K("rden")], writes=[K("attn%d" % kv)])
            ss3, rs3 = small[:, 8:9], small[:, 9:10]
            P.op("act", lambda e: e.activation(out=junk[:, 0:512], in_=attn, func=AF.Square, accum_out=ss3),
                 reads=[K("attn0"), K("attn1")], writes=[K("junk"), K("ss3")])
            P.op("act", lambda e: e.activation(out=rs3, in_=ss3, func=AF.Sqrt, scale=1.0 / 512, bias=epsc),
                 reads=[K("ss3"), K("epsc")], writes=[K("rs3")])
            P.op("dve", lambda e: e.reciprocal(out=rs3, in_=rs3), reads=[K("rs3")], writes=[K("rs3")])
            P.op("dve", lambda e: e.tensor_scalar(out=mix[:, 0:512], in0=attn, scalar1=rs3, scalar2=None, op0=OP.mult),
                 reads=[K("attn0"), K("attn1"), K("rs3")], writes=[K("mixa")])
            P.capture = None
            P.merged(cap_ssd, cap_att, speed=[1.0, 0.6])
            if dbg:
                P.dma("sp", lambda e: e.dma_start(out=dbg_mix[ti * 128:(ti + 1) * 128, :], in_=mix), reads=[K("mixa"), K("mixs")], writes=[("dram", "dbgm")], cls="dbgm")
            ptm = pbank(0, BF16)
            for k in range(8):
                P.op("pe", lambda e, k=k: e.transpose(out=ptm[:, k * 128:(k + 1) * 128], in_=mix[:, k * 128:(k + 1) * 128],
                                                      identity=ident_b),
                     reads=[K("mixa"), K("mixs"), K("identb")], writes=[PS(0)])
            P.op("act", lambda e: e.copy(out=mixT, in_=ptm.rearrange("p (k t) -> p k t", k=8)),
                 reads=[PS(0)], writes=[K("mixT")])
            for half in range(2):
                b = 5 + 2 * half
                for k in range(8):
                    P.op("pe", lambda e, half=half, b=b, k=k: e.matmul(
                        out=pbank(b), lhsT=mixT[:, k, :], rhs=wo_sb[:, k, half * 512:(half + 1) * 512],
                        start=(k == 0), stop=(k == 7)),
                        reads=[K("mixT"), WOKEYS[k]], writes=[PS(b)])
                P.op("dve", lambda e, half=half, b=b: e.tensor_tensor(
                    out=h[:, ti, half * 512:(half + 1) * 512], in0=pbank(b), in1=h[:, ti, half * 512:(half + 1) * 512],
                    op=OP.add),
                    reads=[PS(b), srckey], writes=[srckey])

    npipe = max(npre - 1, 0)
    if npipe > 0:
        def cap(fn):
            lst = []
            P.capture = lst
            fn()
            P.capture = None
            return lst

        b2_pending = {}
        for k in range(npipe + 2):
            fl = cap(lambda: mixer_tile(k, "front")) if k < npipe else []
            b1 = []
            if 0 <= k - 1 < npipe:
                full = cap(lambda: mixer_tile(k - 1, "back"))
                mi = [i for i, it in enumerate(full) if it[0] == "marker"]
                assert len(mi) == 1
                b1 = full[:mi[0]]
                b2_pending[k - 1] = full[mi[0] + 1:]
            b2 = b2_pending.pop(k - 2, [])
            P.merged(fl, b1, b2)
    for slot in range(npipe, npre + nt):
        P.stage(10 + slot)
        mixer_tile(slot)
    P.stage(100)

    if dbg:
        for ti in range(nt):
            P.dma("sp", lambda e, ti=ti: e.dma_start(out=dbg_h[ti * 128:(ti + 1) * 128, :], in_=h[:, ti, :]),
                  reads=[K("h%d" % ti)], writes=[("dram", "dbg%d" % ti)], force=True, cls="dbg")

    P.barrier()
    P.stage(101)
    off[0] = persist_end
    xn2T = alloc([8, ntok], BF16)
    qTp = alloc([8, ntok], BF16)
    ntau = alloc([nt, 8], F32)
    nL = alloc([nt, 8], F32)
    fng = alloc([1024], F32)
    k12 = alloc([2, 128], BF16)
    p2_base = off[0]
    wq_sb = alloc([8, 1024], BF16)
    sc = alloc([8, 2, 128], F32)
    work16 = alloc([16, 128], F32)
    work8 = alloc([8, 256], F32)
    m16 = alloc([16, 16], F32)
    cand = alloc([8, 256], F32)
    c16 = alloc([8, 16], F32)
    d16 = alloc([8, 16], F32)
    zsum = alloc([8], F32)
    junk2 = alloc([1024], BF16)
    xs2 = alloc([1024], BF16)
    p2a_end = off[0]

    P.dma("sp", lambda e: e.dma_start(out=fng, in_=fng_d), writes=[K("fng")], cls="c0")
    P.dma("pool", lambda e: e.dma_start(out=k12, in_=k12_d), writes=[K("k12")], cls="c1")
    P.dma("pool", lambda e: e.dma_start(out=wq_sb, in_=wq_d), writes=[K("wq%d" % k) for k in range(8)], cls="c2")
    WQK = [K("wq%d" % k) for k in range(8)]

    for ti in range(nt):
        ss, rs = small[:, 0:1], small[:, 1:2]
        src, srckey = h[:, ti, :], K("h%d" % ti)
        P.op("act", lambda e, src=src: e.activation(out=junk2, in_=src, func=AF.Square, accum_out=ss),
             reads=[srckey], writes=[K("junk2"), K("ss")])
        P.op("act", lambda e: e.activation(out=rs, in_=ss, func=AF.Sqrt, scale=1.0 / 1024, bias=epsc),
             reads=[K("ss"), K("epsc")], writes=[K("rs")])
        P.op("dve", lambda e: e.reciprocal(out=rs, in_=rs), reads=[K("rs")], writes=[K("rs")])
        P.op("dve", lambda e, src=src: e.tensor_scalar(out=xs2, in0=src, scalar1=rs, scalar2=None, op0=OP.mult),
             reads=[srckey, K("rs")], writes=[K("xs2")])
        pt = pbank(0, BF16)
        for k in range(8):
            P.op("pe", lambda e, k=k: e.transpose(out=pt[:, k * 128:(k + 1) * 128], in_=xs2[:, k * 128:(k + 1) * 128],
                                                  identity=ident_b),
                 reads=[K("xs2"), K("identb")], writes=[PS(0)])
        P.op("dve", lambda e, ti=ti: e.tensor_tensor(out=xn2T[:, :, ti * 128:(ti + 1) * 128],
                                                     in0=pt.rearrange("p (k t) -> p k t", k=8),
                                                     in1=g_ffn.unsqueeze(2).broadcast_to([128, 8, 128]), op=OP.mult),
             reads=[PS(0), K("cpart")], writes=[K("xn2T%d" % ti)])
    XNK = [K("xn2T%d" % ti) for ti in range(nt)]
    ngrp = (ntok + 511) // 512
    for tg in range(ngrp):
        t0 = tg * 512
        tn = min(512, ntok - t0)
        for hh in range(8):
            b = 1 + (hh % 2)
            for k in range(8):
                P.op("pe", lambda e, hh=hh, k=k, b=b, t0=t0, tn=tn: e.matmul(
                    out=pbank(b)[:, 0:tn], lhsT=wq_sb[:, k, hh * 128:(hh + 1) * 128], rhs=xn2T[:, k, t0:t0 + tn],
                    start=(k == 0), stop=(k == 7)),
                    reads=[WQK[k]] + XNK[tg * 4:tg * 4 + 4], writes=[PS(b)])
            P.op("act", lambda e, hh=hh, b=b, t0=t0, tn=tn: e.copy(out=qTp[:, hh, t0:t0 + tn], in_=pbank(b)[:, 0:tn]),
                 reads=[PS(b)], writes=[K("qTp%d" % tg)])

    def scores(ti, banks):
        tg = ti // 4
        for hh in range(8):
            b = banks[hh // 2]
            for hf in range(2):
                o = ((hh % 2) * 2 + hf) * 128
                P.op("pe", lambda e, hh=hh, hf=hf, b=b, o=o: e.matmul(
                    out=pbank(b)[:, o:o + 128], lhsT=qTp[:, hh, ti * 128:(ti + 1) * 128],
                    rhs=k12[:, hf, :], start=True, stop=True),
                    reads=[K("qTp%d" % tg), K("k12")], writes=[PS(b)])

    for ti in range(nt):
        scores(ti, [3, 4, 5, 6])
        for j in range(4):
            P.op("act", lambda e, j=j: e.copy(out=sc[:, 2 * j:2 * j + 2, :, :],
                                              in_=pbank(3 + j).rearrange("p (a b c) -> p a b c", a=2, b=2)),
                 reads=[PS(3 + j)], writes=[K("sc")])
        for i in range(16):
            hh, hf = i // 2, i % 2
            P.op("dve", lambda e, hh=hh, hf=hf, i=i: e.max(out=m16[:, i, 0:8], in_=sc[:, hh, hf, :]),
                 reads=[K("sc")], writes=[K("m16a%d" % i)])
        for i in range(16):
            hh, hf = i // 2, i % 2
            P.op("dve", lambda e, hh=hh, hf=hf, i=i: e.match_replace(out=work16[:, i, :], in_to_replace=m16[:, i, 0:8],
                                                                    in_values=sc[:, hh, hf, :], imm_value=-1e30),
                 reads=[K("sc"), K("m16a%d" % i)], writes=[K("wk%d" % i)])
        for i in range(16):
            P.op("dve", lambda e, i=i: e.max(out=m16[:, i, 8:16], in_=work16[:, i, :]),
                 reads=[K("wk%d" % i)], writes=[K("m16b%d" % i)])
        m4 = m16.rearrange("p (h f) k -> p h f k", f=2)
        P.op("dve", lambda e: e.tensor_tensor(out=cand.rearrange("p h (a b) -> p h a b", a=16),
                                              in0=m4[:, :, 0, :].unsqueeze(3).broadcast_to([128, 8, 16, 16]),
                                              in1=m4[:, :, 1, :].unsqueeze(2).broadcast_to([128, 8, 16, 16]), op=OP.add),
             reads=[K("m16a%d" % i) for i in range(16)] + [K("m16b%d" % i) for i in range(16)], writes=[K("cand")])
        for hh in range(8):
            P.op("dve", lambda e, hh=hh: e.max(out=c16[:, hh, 0:8], in_=cand[:, hh, :]),
                 reads=[K("cand")], writes=[K("c16a%d" % hh)])
        for hh in range(8):
            P.op("dve", lambda e, hh=hh: e.match_replace(out=work8[:, hh, :], in_to_replace=c16[:, hh, 0:8],
                                                         in_values=cand[:, hh, :], imm_value=-1e30),
                 reads=[K("cand"), K("c16a%d" % hh)], writes=[K("wc%d" % hh)])
        for hh in range(8):
            P.op("dve", lambda e, hh=hh: e.max(out=c16[:, hh, 8:16], in_=work8[:, hh, :]), reads=[K("wc%d" % hh)],
                 writes=[K("c16b%d" % hh)])
        C16A = [K("c16a%d" % hh) for hh in range(8)]
        C16B = [K("c16b%d" % hh) for hh in range(8)]
        P.op("dve", lambda e, ti=ti: e.tensor_scalar(out=ntau[:, ti, :], in0=c16[:, :, 15], scalar1=-1e-4, scalar2=None,
                                                     op0=OP.add),
             reads=C16B, writes=[K("ntau")])
        P.op("dve", lambda e: e.tensor_tensor(out=d16, in0=c16, in1=c16[:, :, 0:1].broadcast_to([128, 8, 16]),
                                              op=OP.subtract),
             reads=C16A + C16B, writes=[K("d16")])
        P.op("act", lambda e: e.activation(out=d16, in_=d16, func=AF.Exp), reads=[K("d16")], writes=[K("d16")])
        P.op("dve", lambda e: e.tensor_reduce(out=zsum, in_=d16, axis=AX.X, op=OP.add), reads=[K("d16")], writes=[K("zsum")])
        P.op("act", lambda e: e.activation(out=zsum, in_=zsum, func=AF.Ln), reads=[K("zsum")], writes=[K("zsum")])
        P.op("dve", lambda e, ti=ti: e.scalar_tensor_tensor(out=nL[:, ti, :], in0=zsum, scalar=-1.0, in1=c16[:, :, 0],
                                                            op0=OP.mult, op1=OP.subtract),
             reads=[K("zsum")] + C16A, writes=[K("nL")])

    P.stage(102)
    P.barrier()
    off[0] = p2_base
    ut = [alloc([8, 512], BF16) for _ in range(2)]
    vv = [alloc([4, 1024], BF16) for _ in range(2)]
    kk = [alloc([512], BF16) for _ in range(2)]
    Eb = [alloc([2, 512], BF16) for _ in range(2)]
    Wb = [alloc([8, 512], BF16) for _ in range(2)]
    Gb = alloc([512], BF16)
    gl = alloc([512], BF16)
    actbs = [alloc([512], BF16) for _ in range(3)]
    actTs = [alloc([4, 128], BF16) for _ in range(2)]

    def load_eg(eg):
        ub = eg % 2
        P.dma("pool", lambda e: e.dma_start(out=kk[ub], in_=kk_d[eg]), writes=[K("kk%d" % ub)], cls="k%d" % ub)
        P.dma("pool", lambda e: e.dma_start(out=ut[ub], in_=ut_d[eg]), writes=[K("ut%d" % ub)], cls="u%d" % ub)
        P.dma("pool", lambda e: e.dma_start(out=vv[ub], in_=v_d[eg]), writes=[K("vv%d" % ub)], cls="v%d" % ub)

    def headpair(eg, ti, n, hp):
        ub = eg % 2
        KK = K("kk%d" % ub)
        tg = ti // 4
        W = Wb[n % 2]
        wk = lambda i: K("W%d_%d" % (n % 2, i))
        banks = (3, 4) if hp % 2 == 0 else (5, 6)
        E = Eb[hp % 2]
        for hl in range(2):
            hh = hp * 2 + hl
            bnk = banks[hl]
            ek = K("E%d_%d" % (hp % 2, hl))
            P.op("pe", lambda e, hh=hh, bnk=bnk: e.matmul(out=pbank(bnk), lhsT=qTp[:, hh, ti * 128:(ti + 1) * 128],
                                                          rhs=kk[ub], start=True, stop=True),
                 reads=[K("qTp%d" % tg), KK], writes=[PS(bnk)])
        for hl in range(2):
            hh = hp * 2 + hl
            bnk = banks[hl]
            ek = K("E%d_%d" % (hp % 2, hl))
            P.op("act", lambda e, hh=hh, hl=hl, bnk=bnk, E=E: e.activation(out=E[:, hl, :], in_=pbank(bnk), func=AF.Exp,
                                                                          bias=nL[:, ti, hh:hh + 1]),
                 reads=[PS(bnk), K("nL")], writes=[ek])
            P.op("dve", lambda e, hh=hh, hl=hl, bnk=bnk, E=E: e.scalar_tensor_tensor(
                out=W[:, hh, :], in0=pbank(bnk), scalar=ntau[:, ti, hh:hh + 1], in1=E[:, hl, :],
                op0=OP.is_ge, op1=OP.mult),
                reads=[PS(bnk), K("ntau"), ek], writes=[wk(hh)])

    def amat(eg, ti, n):
        ub = eg % 2
        UK = K("ut%d" % ub)
        W = Wb[n % 2]
        wk = lambda i: K("W%d_%d" % (n % 2, i))
        for k in range(8):
            P.op("pe", lambda e, k=k: e.matmul(out=pbank(1), lhsT=xn2T[:, k, ti * 128:(ti + 1) * 128],
                                               rhs=ut[ub][:, k, :], start=(k == 0), stop=(k == 7)),
                 reads=[XNK[ti], UK], writes=[PS(1)])
        P.op("pool", lambda e: e.tensor_tensor(out=W[:, 0:2, :], in0=W[:, 0:2, :], in1=W[:, 2:4, :], op=OP.add),
             reads=[wk(0), wk(1), wk(2), wk(3)], writes=[wk(0), wk(1)])

    def tree(eg, ti, n):
        actb = actbs[n % 3]
        AK = K("actb%d" % (n % 3))
        W = Wb[n % 2]
        wk = lambda i: K("W%d_%d" % (n % 2, i))
        P.op("act", lambda e: e.activation(out=gl, in_=pbank(1), func=AF.Gelu), reads=[PS(1)], writes=[K("gl")])
        P.op("dve", lambda e: e.tensor_tensor(out=W[:, 4:6, :], in0=W[:, 4:6, :], in1=W[:, 6:8, :], op=OP.add),
             reads=[wk(4), wk(5), wk(6), wk(7)], writes=[wk(4), wk(5)])
        P.op("dve", lambda e: e.tensor_tensor(out=W[:, 0:2, :], in0=W[:, 0:2, :], in1=W[:, 4:6, :], op=OP.add),
             reads=[wk(0), wk(1), wk(4), wk(5)], writes=[wk(0), wk(1)])
        P.op("dve", lambda e: e.tensor_tensor(out=Gb, in0=W[:, 0, :], in1=W[:, 1, :], op=OP.add),
             reads=[wk(0), wk(1)], writes=[K("Gb")])
        P.op("dve", lambda e: e.tensor_tensor(out=actb, in0=gl, in1=Gb, op=OP.mult),
             reads=[K("gl"), K("Gb")], writes=[AK])

    def tpose(eg, ti, n):
        actb = actbs[n % 3]
        actT = actTs[n % 2]
        AK = K("actb%d" % (n % 3))
        TK = K("actT%d" % (n % 2))
        ptb = pbank(2, BF16)
        for et in range(4):
            P.op("pe", lambda e, et=et: e.transpose(out=ptb[:, et * 128:(et + 1) * 128],
                                                    in_=actb[:, et * 128:(et + 1) * 128], identity=ident_b),
                 reads=[AK, K("identb")], writes=[PS(2)])
        P.op("act", lambda e: e.copy(out=actT, in_=ptb[:, 0:512].rearrange("p (a t) -> p a t", a=4)),
             reads=[PS(2)], writes=[TK])

    def vmat_pe(eg, ti, n):
        ub = eg % 2
        VK = K("vv%d" % ub)
        actT = actTs[n % 2]
        TK = K("actT%d" % (n % 2))
        for et in range(4):
            for half in range(2):
                bnk = 7 if half == 0 else 0
                P.op("pe", lambda e, half=half, bnk=bnk, et=et: e.matmul(
                    out=pbank(bnk), lhsT=actT[:, et, :], rhs=vv[ub][:, et, half * 512:(half + 1) * 512],
                    start=(et == 0), stop=(et == 3)),
                    reads=[TK, VK], writes=[PS(bnk)])

    def hadd(eg, ti, n, half):
        bnk = 7 if half == 0 else 0
        P.op("dve", lambda e: e.tensor_tensor(
            out=h[:, ti, half * 512:(half + 1) * 512], in0=pbank(bnk), in1=h[:, ti, half * 512:(half + 1) * 512],
            op=OP.add),
            reads=[PS(bnk), K("h%d" % ti)], writes=[K("h%d" % ti)])

    load_eg(0)
    if neg > 1:
        load_eg(1)
    steps = [(eg, ti) for eg in range(neg) for ti in range(nt)]
    ns = len(steps)
    for n in range(-1, ns + 1):
        nxt = steps[n + 1] + (n + 1,) if n + 1 < ns else None
        cur = steps[n - 1] + (n - 1,) if 0 <= n - 1 < ns else None
        if nxt:
            headpair(*nxt, 0)
            headpair(*nxt, 1)
        if cur:
            tpose(*cur)
        if nxt:
            amat(*nxt)
            headpair(*nxt, 2)
            headpair(*nxt, 3)
        if cur:
            vmat_pe(*cur)
        if nxt:
            tree(*nxt)
        if cur:
            hadd(*cur, 0)
            hadd(*cur, 1)
        if cur:
            eg, ti = cur[0], cur[1]
            if ti == nt - 1 and eg + 2 < neg:
                load_eg(eg + 2)

    P.stage(103)
    P.barrier()
    off[0] = p2_base
    osbs = [alloc([1024], F32) for _ in range(2)]
    for ti in range(nt):
        osb = osbs[ti % 2]
        ok = K("osb%d" % (ti % 2))
        ss, rs = small[:, 0:1], small[:, 1:2]
        src, srckey = h[:, ti, :], K("h%d" % ti)
        P.op("act", lambda e, src=src, osb=osb: e.activation(out=osb, in_=src, func=AF.Square, accum_out=ss),
             reads=[srckey], writes=[ok, K("ss")])
        P.op("act", lambda e: e.activation(out=rs, in_=ss, func=AF.Sqrt, scale=1.0 / 1024, bias=epsc),
             reads=[K("ss"), K("epsc")], writes=[K("rs")])
        P.op("dve", lambda e: e.reciprocal(out=rs, in_=rs), reads=[K("rs")], writes=[K("rs")])
        P.op("dve", lambda e, src=src, osb=osb: e.scalar_tensor_tensor(out=osb, in0=src, scalar=rs, in1=fng,
                                                                      op0=OP.mult, op1=OP.mult),
             reads=[srckey, K("rs"), K("fng"), ok], writes=[ok])
        P.dma("sp", lambda e, ti=ti, osb=osb: e.dma_start(out=out_d[ti * 128:(ti + 1) * 128, :], in_=osb),
              reads=[ok], writes=[("dram", "out%d" % ti)], cls="o%d" % (ti % 2))
    P.op("sp", None, reads=[("dram", "out%d" % ti) for ti in range(nt)] +
         ([("dram", "dbg%d" % ti) for ti in range(nt)] if dbg else []), writes=[("fin",)], force=True)
    P.op("act", None, reads=[("fin",)], writes=[], force=True)

    sems = {e: enter(nc.semaphore("s_" + e)) for e in Prog.ENG}
    dsems = {c: enter(nc.semaphore("d_" + c)) for c in sorted(P.dcount.keys())}
    block = enter(nc.Block())
    P.emit(nc, sems, dsems, block)
    for cm in reversed(ctx):
        cm.__exit__(None, None, None)
    return nc


def _consts():
    ident = np.eye(128, dtype=np.float32)
    idx = np.arange(128)
    tri_le = (idx[:, None] <= idx[None, :]).astype(np.float32)
    tri_gt = (idx[:, None] > idx[None, :]).astype(np.float32)
    ones = np.ones((128, 128), np.float32)
    cst = np.stack([ident, tri_le, tri_gt, ones], axis=1)
    slopes = np.exp2(-(8.0 / 8) * np.arange(1, 9)).astype(np.float32)
    s = idx[:, None]
    t = idx[None, :]
    bias = np.full((3, 128, 2, 4, 128), NEGBIG, np.float32)
    for kv in range(2):
        for g in range(4):
            sl = slopes[kv * 4 + g] * 8.0
            dprev = 128 + t - s
            bp = np.where(dprev < 128, -sl * dprev, NEGBIG)
            dcur = t - s
            bc = np.where(dcur >= 0, -sl * dcur, NEGBIG)
            bias[1, :, kv, g, :] = bp
            bias[2, :, kv, g, :] = bc
    return np.ascontiguousarray(cst), slopes, bias.reshape(3, 128, 2, 512)


def prep_inputs(x, norm_mix_g, w_in, attn_sinks, attn_out_g, conv_w, conv_b, dt_bias, a_log, d_skip,
                ssm_norm_g, w_out, norm_ffn_g, peer_wq, peer_sub_keys, peer_u, peer_v, final_norm_g,
                nt=NT, npre=NPRE, neg=NEG):
    f = np.float32
    x = np.asarray(x, f)
    W = np.asarray(w_in[0], f)
    q, k, v, z, xbc, dt = W[:, 0:512], W[:, 512:640], W[:, 640:768], W[:, 768:1280], W[:, 1280:2304], W[:, 2304:2312]
    zz = np.zeros((1024, 64), f)
    wcat = np.concatenate([q, k[:, 0:64], zz, zz, k[:, 0:64], k[:, 64:128], zz, zz, k[:, 64:128], xbc, z, v, dt], axis=1)
    assert wcat.shape[1] == WF + WT
    lay = lambda m: np.ascontiguousarray(m.reshape(8, 128, m.shape[1]).transpose(1, 0, 2))
    col = lambda vec: np.ascontiguousarray(np.asarray(vec, f).reshape(8, 128).T)
    rep = lambda vec: np.broadcast_to(np.asarray(vec, f)[None, :], (128, len(vec)))
    cw = np.asarray(conv_w[0], f)
    cpart = np.concatenate([col(norm_mix_g[0]), col(norm_ffn_g[0]),
                            col(np.concatenate([attn_out_g[0], ssm_norm_g[0]])),
                            np.concatenate([col(cw[i]) for i in range(4)], axis=1), col(conv_b[0]),
                            np.zeros((128, 8), f)], axis=1)
    cvec = np.concatenate([rep(attn_sinks[0]), rep(dt_bias[0]), rep(a_log[0]), rep(d_skip[0])], axis=1)
    cst, _, abias = _consts()
    sk = np.asarray(peer_sub_keys[0], f)
    k12 = np.zeros((128, 2, 128), f)
    k12[0:64, 0, :] = sk[0].T
    k12[64:128, 1, :] = sk[1].T
    kkh = np.empty((NEG, 128, 4, 128), f)
    kkh[:, 0:64] = sk[0].reshape(NEG, 4, 64).transpose(0, 2, 1)[:, :, :, None]
    kkh[:, 64:128] = sk[1].T[None, :, None, :]
    kkh = np.ascontiguousarray(kkh.reshape(NEG, 128, 512))
    U = np.asarray(peer_u[0], f)
    ut = np.ascontiguousarray(U.T.reshape(8, 128, NEG, 512).transpose(2, 1, 0, 3))
    V = np.asarray(peer_v[0], f)
    vv = np.ascontiguousarray(V.reshape(NEG, 4, 128, 1024).transpose(0, 2, 1, 3))
    shared = dict(w_in=lay(wcat), w_out=lay(np.asarray(w_out[0], f)), wq=lay(np.asarray(peer_wq[0], f)),
                  k12=np.ascontiguousarray(k12), ut=ut[:neg], vv=vv[:neg], kk=kkh[:neg], cvec=np.ascontiguousarray(cvec),
                  cpart=np.ascontiguousarray(cpart), fng=np.ascontiguousarray(rep(final_norm_g)), cst=cst)
    in_maps = []
    tiles_per_seq = x.shape[1] // 128
    segs = tiles_per_seq // nt
    for c in range(NCORES):
        b, s = c // segs, c % segs
        t0 = s * nt
        xo = x[b, t0 * 128:(t0 + nt) * 128]
        xp = np.zeros((max(npre, 1) * 128, 1024), f)
        pf = np.zeros((128, max(npre, 1)), f)
        for j in range(npre):
            gt = t0 - npre + j
            if gt >= 0:
                xp[j * 128:(j + 1) * 128] = x[b, gt * 128:(gt + 1) * 128]
                pf[:, j] = 1.0
        ab = abias.copy()
        ab[0] = abias[1] if s > 0 else NEGBIG
        m = dict(shared)
        m.update(x_own=np.ascontiguousarray(xo), x_pre=xp, pflag=pf, abias=np.ascontiguousarray(ab))
        in_maps.append(m)
    return in_maps


_NC_CACHE = {}


def kernel(**inputs):
    in_maps = prep_inputs(**inputs)
    if "nc" not in _NC_CACHE:
        _NC_CACHE["nc"] = build()
    nc = _NC_CACHE["nc"]
    res = run_bass_kernel_spmd(nc, in_maps, core_ids=list(range(NCORES)))
    outs = [np.asarray(r["out"], np.float32) for r in res.results]
    x = inputs["x"]
    return np.concatenate(outs, axis=0).reshape(x.shape).astype(np.float32)
```

```python
import numpy as np
import concourse.bass as bass
import concourse.mybir as mybir
from concourse.bass_utils import run_bass_kernel_spmd

F32 = mybir.dt.float32
BF16 = mybir.dt.bfloat16
U8 = mybir.dt.uint8
AF = mybir.ActivationFunctionType
OP = mybir.AluOpType
AX = mybir.AxisListType

EPS = 1e-6
NCORES = 8
NT = 16
NPRE = 48
NEG = 32
WF = 2048
WT = 648
NEGBIG = -240000.0


class Prog:
    ENG = ["pe", "act", "dve", "pool", "sp"]

    def __init__(self):
        self.ops = []
        self.last_w = {}
        self.readers = {}
        self.bar = None
        self.frozen = False
        self.capture = None
        self.last_dma = {}
        self.dcount = {}
        self.limit = 10 ** 9

    def stage(self, n):
        if n > self.limit:
            self.frozen = True

    def _add(self, eng, fn, reads, writes, dma, force=False):
        if self.frozen and not force:
            return -1
        psr = tuple(k for k in reads if k[0] == "ps")
        if psr:
            reads = tuple(k for k in reads if k[0] != "ps")
            writes = tuple(writes) + tuple(k for k in psr if k not in writes)
        idx = len(self.ops)
        deps = set()
        for k in reads:
            if k in self.last_w:
                deps.add(self.last_w[k])
        for k in writes:
            if k in self.last_w:
                deps.add(self.last_w[k])
            for r in self.readers.get(k, ()):
                deps.add(r)
        if self.bar is not None:
            deps.add(self.bar)
        deps.discard(idx)
        self.ops.append(dict(eng=eng, fn=fn, deps=deps, dma=dma, inc=False))
        for k in reads:
            self.readers.setdefault(k, []).append(idx)
        for k in writes:
            self.last_w[k] = idx
            self.readers[k] = []
        return idx

    def op(self, eng, fn, reads=(), writes=(), force=False):
        if self.capture is not None:
            self.capture.append(("op", eng, fn, tuple(reads), tuple(writes), force, None))
            return -2
        return self._add(eng, fn, tuple(reads), tuple(writes), False, force)

    def replay(self, item):
        kind, eng, fn, reads, writes, force, cls = item
        if kind == "op":
            return self._add(eng, fn, reads, writes, False, force)
        return self.dma(eng, fn, reads, writes, force, cls)

    def merged(self, *lists, speed=None):
        if speed is None:
            speed = [1.0] * len(lists)
        keep = [i for i, l in enumerate(lists) if l]
        speed = [speed[i] for i in keep]
        lists = [lists[i] for i in keep]
        pos = [0] * len(lists)
        while True:
            best, bf = -1, 1e9
            for i, l in enumerate(lists):
                if pos[i] < len(l):
                    f = (pos[i] + 0.5) / len(l) * speed[i]
                    if f < bf:
                        best, bf = i, f
            if best < 0:
                break
            self.replay(lists[best][pos[best]])
            pos[best] += 1

    def dma(self, eng, fn, reads=(), writes=(), force=False, cls="c0"):
        if self.capture is not None:
            self.capture.append(("dma", eng, fn, tuple(reads), tuple(writes), force, cls))
            return -2
        idx = self._add(eng, fn, tuple(reads), tuple(writes), True, force)
        if idx < 0:
            return idx
        o = self.ops[idx]
        if cls in self.last_dma:
            o["deps"].add(self.last_dma[cls])
        self.last_dma[cls] = idx
        self.dcount[cls] = self.dcount.get(cls, 0) + 1
        o["cls"] = cls
        o["val"] = ("d", cls, 16 * self.dcount[cls])
        return idx

    def barrier(self):
        keys = tuple(set(self.last_w.keys()) | set(self.readers.keys()))
        last = None
        for e in self.ENG:
            last = self._add(e, None, (), keys, False)
        if last is not None and last >= 0:
            self.bar = last

    def emit(self, nc, sems, dsems, block):
        ops = self.ops
        for o in ops:
            for d in o["deps"]:
                ops[d]["inc"] = True
        cnt = {e: 0 for e in self.ENG}
        dcnt = {e: 0 for e in self.ENG}
        for o in ops:
            e = o["eng"]
            if o["dma"]:
                pass
            else:
                if o["fn"] is None and o["inc"]:
                    pass
                if o["inc"]:
                    cnt[e] += 1
                o["val"] = ("c", e, cnt[e])
        per = {e: [] for e in self.ENG}
        for i, o in enumerate(ops):
            per[o["eng"]].append(i)

        def run(eng_name, eng):
            known = {}
            for i in per[eng_name]:
                o = ops[i]
                need = {}
                for d in o["deps"]:
                    kind, e, v = ops[d]["val"]
                    if kind == "c" and not ops[d]["inc"]:
                        continue
                    key = (kind, e)
                    if v > need.get(key, 0):
                        need[key] = v
                for key, v in need.items():
                    if known.get(key, 0) >= v:
                        continue
                    known[key] = v
                    s = dsems[key[1]] if key[0] == "d" else sems[key[1]]
                    eng.wait_ge(s, v)
                if o["fn"] is None:
                    if o["inc"]:
                        eng.nop().then_inc(sems[eng_name], 1)
                    continue
                ins = o["fn"](eng)
                if o["dma"]:
                    ins.then_inc(dsems[o["cls"]], 16)
                elif o["inc"]:
                    ins.then_inc(sems[eng_name], 1)

        @block.tensor
        def _(e):
            run("pe", e)

        @block.scalar
        def _(e):
            run("act", e)

        @block.vector
        def _(e):
            run("dve", e)

        @block.gpsimd
        def _(e):
            run("pool", e)

        @block.sync
        def _(e):
            run("sp", e)


def build(nt=NT, npre=NPRE, neg=NEG, dbg=False, limit=10 ** 9):
    nc = bass.Bass("TRN2", target_bir_lowering=False)
    P = Prog()
    P.limit = limit
    ntok = nt * 128

    def din(name, shape, dt=F32):
        return nc.dram_tensor(name, list(shape), dt, kind="ExternalInput").ap()

    x_own = din("x_own", [ntok, 1024])
    x_pre = din("x_pre", [max(npre, 1) * 128, 1024])
    pflag_d = din("pflag", [128, max(npre, 1)])
    w_d = din("w_in", [128, 8, WF + WT])
    wo_d = din("w_out", [128, 8, 1024])
    wq_d = din("wq", [128, 8, 1024])
    k12_d = din("k12", [128, 2, 128])
    ut_d = din("ut", [neg, 128, 8, 512])
    kk_d = din("kk", [neg, 128, 512])
    v_d = din("vv", [neg, 128, 4, 1024])
    cvec_d = din("cvec", [128, 32])
    cpart_d = din("cpart", [128, 72])
    fng_d = din("fng", [128, 1024])
    cst_d = din("cst", [128, 4, 128])
    bias_d = din("abias", [3, 128, 2, 512])
    out_d = nc.dram_tensor("out", [ntok, 1024], F32, kind="ExternalOutput").ap()
    if dbg:
        dbg_h = nc.dram_tensor("dbg_h", [ntok, 1024], F32, kind="ExternalOutput").ap()
        dbg_mix = nc.dram_tensor("dbg_mix", [ntok, 1024], BF16, kind="ExternalOutput").ap()

    ctx = []

    def enter(cm):
        ctx.append(cm)
        return cm.__enter__()

    ARENA = 204 * 1024
    arena = enter(nc.sbuf_tensor("arena", [128, ARENA], U8))
    psum = enter(nc.psum_tensor("psum", [128, 4096], F32))
    off = [0]

    def alloc(shape, dt, at=None):
        n = 1
        for s in shape:
            n *= s
        nb = n * (4 if dt == F32 else 2)
        nb_al = (nb + 31) // 32 * 32
        if at is None:
            o = off[0]
            off[0] += nb_al
        else:
            o = at
        assert o + nb_al <= ARENA, (o, nb_al)
        v = arena[:, o:o + nb].bitcast(dt)
        if len(shape) == 2:
            v = v.rearrange("p (a b) -> p a b", a=shape[0])
        elif len(shape) == 3:
            v = v.rearrange("p (a b c) -> p a b c", a=shape[0], b=shape[1])
        return v

    def pbank(b, dt=F32):
        v = psum[:, b * 512:(b + 1) * 512]
        if dt != F32:
            v = v.bitcast(dt)
        return v

    def PS(b):
        return ("ps", b)

    h = alloc([nt, 1024], F32)
    cst = alloc([4, 128], F32)
    ident_f, tri_le, tri_gt, ones_f = cst[:, 0, :], cst[:, 1, :], cst[:, 2, :], cst[:, 3, :]
    ident_b = alloc([128], BF16)
    cvec = alloc([32], F32)
    cpart = alloc([72], F32)
    cder = alloc([32], F32)
    small = alloc([64], F32)
    persist_end = off[0]

    g_mix, g_ffn, gcat = cpart[:, 0:8], cpart[:, 8:16], cpart[:, 16:24]
    convw = cpart[:, 24:56].rearrange("p (k c) -> p k c", k=4)
    convb = cpart[:, 56:64]
    sinks_b, dtb_b, alog_b, dskip_b = cvec[:, 0:8], cvec[:, 8:16], cvec[:, 16:24], cvec[:, 24:32]
    esink, negA = cder[:, 0:8], cder[:, 8:16]

    w_sb = alloc([8, WF + WT], BF16)
    wo_sb = alloc([8, 1024], BF16)
    abias = alloc([3, 2, 512], BF16)
    pflag = alloc([max(npre, 1)], F32)
    xtmp = [alloc([1024], F32) for _ in range(2)]
    junk = alloc([1024], BF16)
    xs_bf = alloc([1024], BF16)
    xnT = alloc([8, 128], BF16)
    qT = alloc([4, 128], BF16)
    kT = [alloc([2, 2, 128], BF16) for _ in range(2)]
    vext = [alloc([2, 128], BF16) for _ in range(2)]
    xpre = alloc([8, 131], F32)
    cacc = alloc([8, 128], F32)
    xcTs = [alloc([6, 128], F32) for _ in range(2)]
    bcT = alloc([4, 128], BF16)
    zs = alloc([512], F32)
    dtss = [alloc([64], F32) for _ in range(2)]
    dec = alloc([3, 8], F32)
    xsk = alloc([512], F32)
    xdt = alloc([512], BF16)
    xdec = alloc([512], BF16)
    btok = alloc([2, 128], BF16)
    state = alloc([512], F32)
    stateb = alloc([512], BF16)
    stmp = alloc([512], F32)
    rhsA = alloc([8, 128], F32)
    cbm = alloc([2, 128], F32)
    cblt = alloc([8, 128], BF16)
    t1 = alloc([512], F32)
    t2 = alloc([512], F32)
    pT_sb = [alloc([512], BF16) for _ in range(4)]
    attn = alloc([512], F32)
    mix = alloc([1024], BF16)
    mixT = alloc([8, 128], BF16)
    ytok = [alloc([512], F32) for _ in range(2)]
    phase1_end = off[0]

    K = lambda name: ("sb", name)

    if limit < 10 ** 9:
        for ti in range(nt):
            P.op("pool", lambda e, ti=ti: e.memset(h[:, ti, :], 0.0), writes=[K("h%d" % ti)], force=True)
    P.stage(0.05)
    P.dma("sp", lambda e: e.dma_start(out=cst, in_=cst_d), writes=[K("cst")])
    P.dma("sp", lambda e: e.dma_start(out=cvec, in_=cvec_d), writes=[K("cvec")])
    P.dma("sp", lambda e: e.dma_start(out=cpart, in_=cpart_d), writes=[K("cpart")])
    P.dma("sp", lambda e: e.dma_start(out=pflag, in_=pflag_d), writes=[K("pflag")])
    P.stage(0.1)
    P.dma("pool", lambda e: e.dma_start(out=w_sb.rearrange("p k (a b) -> p (k a) b", b=337),
                                        in_=w_d.rearrange("p k (a b) -> p (k a) b", b=337)),
          writes=[K("w_sb%d" % k) for k in range(8)], cls="c1")
    P.stage(0.2)
    P.dma("pool", lambda e: e.dma_start(out=wo_sb, in_=wo_d), writes=[K("wo_sb%d" % k) for k in range(8)], cls="c2")
    P.stage(0.3)
    for j in range(3):
        P.dma("pool", lambda e, j=j: e.dma_start(out=abias[:, j, :, :], in_=bias_d[j]),
              writes=[K("abias")] if j == 2 else [K("abias%d" % j)], cls="c3")
    P.stage(0.4)
    P.op("dve", lambda e: e.tensor_copy(out=ident_b, in_=ident_f), reads=[K("cst")], writes=[K("identb")])
    for k in range(8):
        P.op("pool", lambda e, k=k: e.tensor_scalar(out=wo_sb[:, k, :], in0=wo_sb[:, k, :], scalar1=gcat[:, k:k + 1],
                                                    scalar2=None, op0=OP.mult),
             reads=[K("cpart"), K("wo_sb%d" % k)], writes=[K("wo_sb%d" % k)])
    WKEYS = [K("w_sb%d" % k) for k in range(8)]
    WOKEYS = [K("wo_sb%d" % k) for k in range(8)]
    P.stage(0.5)
    P.op("act", lambda e: e.activation(out=esink, in_=sinks_b, func=AF.Exp), reads=[K("cvec")], writes=[K("esink")])
    P.op("act", lambda e: e.activation(out=negA, in_=alog_b, func=AF.Exp), reads=[K("cvec")], writes=[K("negA")])
    P.op("dve", lambda e: e.tensor_scalar(out=negA, in0=negA, scalar1=-1.0, scalar2=None, op0=OP.mult),
         reads=[K("negA")], writes=[K("negA")])
    P.stage(0.6)
    P.op("dve", lambda e: e.memset(state, 0.0), writes=[K("state")])
    P.op("dve", lambda e: e.memset(stateb, 0.0), writes=[K("stateb")])
    P.op("dve", lambda e: e.memset(xpre, 0.0), writes=[K("xpre")])
    for par in range(2):
        P.op("pool", lambda e, par=par: e.memset(vext[par], 1.0), writes=[K("vext%d" % par)])
        P.op("pool", lambda e, par=par: e.memset(kT[par], 0.0), writes=[K("kT%d" % par)])

    P.stage(1)
    def norm_transpose(src, srckey, gcol, dst, dstkey, tag):
        ss = small[:, 0:1]
        rs = small[:, 1:2]
        P.op("act", lambda e: e.activation(out=junk, in_=src, func=AF.Square, accum_out=ss),
             reads=[srckey], writes=[K("junk"), K("ss")])
        P.op("act", lambda e: e.activation(out=rs, in_=ss, func=AF.Sqrt, scale=1.0 / 1024, bias=epsc),
             reads=[K("ss"), K("epsc")], writes=[K("rs")])
        P.op("dve", lambda e: e.reciprocal(out=rs, in_=rs), reads=[K("rs")], writes=[K("rs")])
        P.op("dve", lambda e: e.tensor_scalar(out=xs_bf, in0=src, scalar1=rs, scalar2=None, op0=OP.mult),
             reads=[srckey, K("rs")], writes=[K("xs_bf")])
        pt = pbank(0, BF16)
        for k in range(8):
            P.op("pe", lambda e, k=k: e.transpose(out=pt[:, k * 128:(k + 1) * 128], in_=xs_bf[:, k * 128:(k + 1) * 128],
                                                  identity=ident_b),
                 reads=[K("xs_bf"), K("identb")], writes=[PS(0)])
        P.op("dve", lambda e: e.tensor_tensor(out=dst, in0=pt.rearrange("p (k t) -> p k t", k=8),
                                              in1=gcol.unsqueeze(2).broadcast_to([128, 8, 128]), op=OP.mult),
             reads=[PS(0), K("cpart")], writes=[dstkey])

    epsc = small[:, 2:3]
    P.op("dve", lambda e: e.memset(epsc, EPS), writes=[K("epsc")])

    ytc = [0]

    def mixer_tile(slot, phase="both"):
        own = slot >= npre
        last_pre = (slot == npre - 1)
        pipe = (not own) and (not last_pre)
        ti = slot - npre
        par = slot % 2
        ppar = 1 - par
        xcT = xcTs[par]
        dts = dtss[par]
        if own:
            src = h[:, ti, :]
            srckey = K("h%d" % ti)
        else:
            src = xtmp[par]
            srckey = K("xtmp%d" % par)
        if own:
            fch = list(range(16))
        elif last_pre:
            fch = list(range(4, 16))
        else:
            fch = list(range(8, 14))
        vlo = WF + 512 if (own or last_pre) else WF + 640
        nv = WF + WT - vlo
        if pipe:
            bxs, bB = (3, 4) if par == 0 else (1, 2)
            bdt, dcol = bB, 256
            bTX, bTB, bct = 5, 6, 7
        else:
            bxs, bB = 3, 4
            bdt, dcol = 6, nv - 8
            bTX, bTB, bct = (5, 6, 6) if own else (1, 2, 4)

        def fdst(cid):
            if pipe:
                return (bxs, cid - 8) if cid < 12 else (bB, cid - 12)
            return 1 + cid // 4, cid % 4

        if phase != "back":
            if own:
                P.dma("sp", lambda e: e.dma_start(out=src, in_=x_own[ti * 128:(ti + 1) * 128, :]), writes=[srckey], cls="xh%d" % (ti % 2))
            else:
                P.dma("sp", lambda e: e.dma_start(out=src, in_=x_pre[slot * 128:(slot + 1) * 128, :]), writes=[srckey], cls="xp%d" % par)
            norm_transpose(src, srckey, g_mix, xnT, K("xnT"), "m")
            groups = {}
            for cid in fch:
                b, j = fdst(cid)
                groups.setdefault(b, []).append((j, cid))
            glist = []
            for b in sorted(groups):
                lst = sorted(groups[b])
                glist.append((b, lst[0][0], lst[0][1], len(lst)))

            def tback(i):
                b, j0, c0, nch = glist[i]
                yb = ytok[(ytc[0] + i) % 2]
                yk = K("ytok%d" % ((ytc[0] + i) % 2))
                for jj in range(nch):
                    P.op("pe", lambda e, b=b, j0=j0, jj=jj, yb=yb: e.transpose(
                        out=pbank(b)[:, (j0 + jj) * 128:(j0 + jj + 1) * 128], in_=yb[:, jj * 128:(jj + 1) * 128],
                        identity=ident_f),
                        reads=[yk, K("cst")], writes=[PS(b)])

            for i, (b, j0, c0, nch) in enumerate(glist):
                for k in range(8):
                    P.op("pe", lambda e, b=b, j0=j0, c0=c0, nch=nch, k=k: e.matmul(
                        out=pbank(b)[:, j0 * 128:(j0 + nch) * 128], lhsT=xnT[:, k, :],
                        rhs=w_sb[:, k, c0 * 128:(c0 + nch) * 128], start=(k == 0), stop=(k == 7)),
                        reads=[K("xnT"), WKEYS[k]], writes=[PS(b)])
                yb = ytok[(ytc[0] + i) % 2]
                yk = K("ytok%d" % ((ytc[0] + i) % 2))
                eng = "act" if (ytc[0] + i) % 2 == 0 else "dve"
                if eng == "act":
                    P.op("act", lambda e, b=b, j0=j0, nch=nch, yb=yb: e.copy(
                        out=yb[:, 0:nch * 128], in_=pbank(b)[:, j0 * 128:(j0 + nch) * 128]),
                        reads=[PS(b)], writes=[yk])
                else:
                    P.op("dve", lambda e, b=b, j0=j0, nch=nch, yb=yb: e.tensor_copy(
                        out=yb[:, 0:nch * 128], in_=pbank(b)[:, j0 * 128:(j0 + nch) * 128]),
                        reads=[PS(b)], writes=[yk])
                if i >= 1:
                    tback(i - 1)
            if own:
                for k in range(8):
                    P.op("pe", lambda e, k=k: e.matmul(out=pbank(5), lhsT=xnT[:, k, :], rhs=w_sb[:, k, WF:WF + 512],
                                                       start=(k == 0), stop=(k == 7)),
                         reads=[K("xnT"), WKEYS[k]], writes=[PS(5)])
            for k in range(8):
                P.op("pe", lambda e, k=k: e.matmul(out=pbank(bdt)[:, dcol + 8 - nv:dcol + 8], lhsT=xnT[:, k, :],
                                                   rhs=w_sb[:, k, vlo:WF + WT], start=(k == 0), stop=(k == 7)),
                     reads=[K("xnT"), WKEYS[k]], writes=[PS(bdt)])
            tback(len(glist) - 1)
            ytc[0] += len(glist)
        if phase == "front":
            return
        P.stage(10 + slot + 0.1)
        if own:
            P.op("act", lambda e: e.copy(out=qT, in_=pbank(1).rearrange("p (c t) -> p c t", c=4)),
                 reads=[PS(1)], writes=[K("qT")])
            P.op("act", lambda e: e.activation(out=zs, in_=pbank(5), func=AF.Silu), reads=[PS(5)], writes=[K("zs")])
        if own or last_pre:
            P.op("act", lambda e: e.copy(out=kT[par], in_=pbank(2).rearrange("p (c f t) -> p c f t", c=2, f=2)),
                 reads=[PS(2)], writes=[K("kT%d" % par)])
            P.op("act", lambda e: e.copy(out=vext[par][:, :, 0:64],
                                         in_=pbank(6)[:, 0:128].rearrange("p (c t) -> p c t", c=2)),
                 reads=[PS(6)], writes=[K("vext%d" % par)])
        dtx, dtv, adt, dtf = dts[:, 0:8], dts[:, 8:16], dts[:, 16:24], dts[:, 24:32]
        P.op("dve", lambda e: e.tensor_tensor(out=dtx, in0=pbank(bdt)[:, dcol:dcol + 8], in1=dtb_b, op=OP.add),
             reads=[PS(bdt), K("cvec")], writes=[K("dtx%d" % par)])
        P.op("act", lambda e: e.activation(out=dtx, in_=dtx, func=AF.Exp), reads=[K("dtx%d" % par)], writes=[K("dtx%d" % par)])
        P.op("act", lambda e: e.activation(out=dtv, in_=dtx, func=AF.Ln, bias=1.0), reads=[K("dtx%d" % par)], writes=[K("dtv%d" % par)])
        P.op("dve", lambda e: e.tensor_tensor(out=adt, in0=dtv, in1=negA, op=OP.mult),
             reads=[K("dtv%d" % par), K("negA")], writes=[K("adt%d" % par)])
        if own:
            dtf_ap, dtfkey = dtv, K("dtv%d" % par)
        else:
            P.op("dve", lambda e: e.tensor_scalar(out=dtf, in0=dtv, scalar1=pflag[:, slot:slot + 1], scalar2=None,
                                                  op0=OP.mult),
                 reads=[K("dtv%d" % par), K("pflag")], writes=[K("dtf%d" % par)])
            dtf_ap, dtfkey = dtf, K("dtf%d" % par)
        P.stage(10 + slot + 0.2)
        cch = list(range(8)) if own else list(range(6))
        ncc = len(cch)
        P.op("act", lambda e: e.copy(out=xpre[:, 0:4, 3:131], in_=pbank(bxs).rearrange("p (c t) -> p c t", c=4)),
             reads=[PS(bxs)], writes=[K("xpre")])
        nb4 = 4 if (own or last_pre) else 2
        P.op("act", lambda e: e.copy(out=xpre[:, 4:4 + nb4, 3:131],
                                     in_=pbank(bB)[:, 0:nb4 * 128].rearrange("p (c t) -> p c t", c=nb4)),
             reads=[PS(bB)], writes=[K("xpre")])
        cap_ssd, cap_att = [], []
        if own:
            P.capture = cap_ssd
        for c in cch:
            P.op("dve", lambda e, c=c: e.tensor_scalar(out=cacc[:, c, :], in0=xpre[:, c, 0:128], scalar1=convw[:, 0, c:c + 1],
                                                       scalar2=convb[:, c:c + 1], op0=OP.mult, op1=OP.add),
                 reads=[K("xpre"), K("cpart")], writes=[K("cacc%d" % c)])
        for kk in range(1, 4):
            for c in cch:
                P.op("dve", lambda e, c=c, kk=kk: e.scalar_tensor_tensor(
                    out=cacc[:, c, :], in0=xpre[:, c, kk:kk + 128], scalar=convw[:, kk, c:c + 1], in1=cacc[:, c, :],
                    op0=OP.mult, op1=OP.add),
                    reads=[K("xpre"), K("cpart"), K("cacc%d" % c)], writes=[K("cacc%d" % c)])
        P.op("pool", lambda e: e.tensor_copy(out=xpre[:, :, 0:3], in_=xpre[:, :, 128:131]),
             reads=[K("xpre")] + [K("cacc%d" % c) for c in cch], writes=[K("xpre")])
        CK = [K("cacc%d" % c) for c in cch]
        P.op("act", lambda e: e.activation(out=xcT, in_=cacc[:, 0:6, :], func=AF.Silu), reads=CK, writes=[K("xcT%d" % par)])
        if own:
            P.op("act", lambda e: e.activation(out=bcT, in_=cacc[:, 4:8, :], func=AF.Silu), reads=CK, writes=[K("bcT")])
        P.stage(10 + slot + 0.3)
        if P.capture is not None and pipe:
            P.capture.append(("marker",))
        for j in range(6):
            b, jj = (bTX, j) if j < 4 else (bTB, j - 4)
            P.op("pe", lambda e, j=j, b=b, jj=jj: e.transpose(out=pbank(b)[:, jj * 128:(jj + 1) * 128], in_=xcT[:, j, :],
                                                              identity=ident_f),
                 reads=[K("xcT%d" % par), K("cst")], writes=[PS(b)])
        for j, lt in enumerate([tri_le, tri_gt, ones_f]):
            P.op("pe", lambda e, j=j, lt=lt: e.matmul(out=pbank(6)[:, 256 + j * 8:256 + (j + 1) * 8], lhsT=lt, rhs=adt,
                                                      start=True, stop=True),
                 reads=[K("adt%d" % par), K("cst")], writes=[PS(6)])
        P.op("act", lambda e: e.activation(out=dec, in_=pbank(6)[:, 256:280].rearrange("p (a b) -> p a b", a=3),
                                           func=AF.Exp),
             reads=[PS(6)], writes=[K("dec")])
        eacs, edst, cdec = dec[:, 0, :], dec[:, 1, :], dec[:, 2, :]
        x3 = pbank(bTX).rearrange("p (h q) -> p h q", h=8)
        P.op("dve", lambda e: e.tensor_tensor(out=xdt.rearrange("p (h q) -> p h q", h=8), in0=x3,
                                              in1=dtf_ap.unsqueeze(2).broadcast_to([128, 8, 64]), op=OP.mult),
             reads=[PS(bTX), dtfkey], writes=[K("xdt")])
        if own:
            P.op("dve", lambda e: e.tensor_tensor(out=xsk.rearrange("p (h q) -> p h q", h=8), in0=x3,
                                                  in1=dskip_b.unsqueeze(2).broadcast_to([128, 8, 64]), op=OP.mult),
                 reads=[PS(bTX), K("cvec")], writes=[K("xsk")])
        P.op("act", lambda e: e.copy(out=btok, in_=pbank(bTB)[:, 0:256].rearrange("p (c t) -> p c t", c=2)),
             reads=[PS(bTB)], writes=[K("btok")])
        P.op("dve", lambda e: e.tensor_tensor(out=xdec.rearrange("p (h q) -> p h q", h=8),
                                              in0=xdt.rearrange("p (h q) -> p h q", h=8),
                                              in1=edst.unsqueeze(2).broadcast_to([128, 8, 64]), op=OP.mult),
             reads=[K("xdt"), K("dec")], writes=[K("xdec")])

        if own:
            P.stage(10 + slot + 0.4)
            for g in range(2):
                P.op("pe", lambda e, g=g: e.matmul(out=pbank(0)[:, g * 128:(g + 1) * 128], lhsT=bcT[:, g, :],
                                                   rhs=bcT[:, 2 + g, :], start=True, stop=True),
                     reads=[K("bcT")], writes=[PS(0)])
            P.op("dve", lambda e: e.tensor_tensor(out=cbm, in0=pbank(0)[:, 0:256].rearrange("p (g l) -> p g l", g=2),
                                                  in1=tri_le.unsqueeze(1).broadcast_to([128, 2, 128]), op=OP.mult),
                 reads=[PS(0), K("cst")], writes=[K("cbm")])
            for hh in range(8):
                P.op("dve", lambda e, hh=hh: e.tensor_scalar(out=rhsA[:, hh, :], in0=tri_le, scalar1=adt[:, hh:hh + 1],
                                                              scalar2=None, op0=OP.mult),
                     reads=[K("cst"), K("adt%d" % par)], writes=[K("rhsA")])
            for half in range(2):
                b = 7 if half == 0 else 5
                P.op("pe", lambda e, half=half, b=b: e.matmul(
                    out=pbank(b), lhsT=tri_gt, rhs=rhsA[:, half * 4:(half + 1) * 4, :], start=True, stop=True),
                    reads=[K("rhsA"), K("cst")], writes=[PS(b)])
                P.op("act", lambda e, b=b: e.activation(out=pbank(b), in_=pbank(b), func=AF.Exp),
                     reads=[PS(b)], writes=[PS(b)])
                P.op("dve", lambda e, half=half, b=b: e.tensor_tensor(
                    out=cblt[:, half * 4:(half + 1) * 4, :], in0=pbank(b).rearrange("p (r l) -> p r l", r=4),
                    in1=cbm[:, half, :].unsqueeze(1).broadcast_to([128, 4, 128]), op=OP.mult),
                    reads=[PS(b), K("cbm")], writes=[K("cblt%d" % half)])
            P.stage(10 + slot + 0.45)
            for hh in range(8):
                P.op("pe", lambda e, hh=hh: e.matmul(out=pbank(7)[:, hh * 64:(hh + 1) * 64], lhsT=cblt[:, hh, :],
                                                     rhs=xdt[:, hh * 64:(hh + 1) * 64], start=True, stop=True),
                     reads=[K("cblt0"), K("cblt1"), K("xdt")], writes=[PS(7)])
            for g in range(2):
                P.op("pe", lambda e, g=g: e.matmul(out=pbank(5)[:, g * 256:(g + 1) * 256], lhsT=bcT[:, 2 + g, :],
                                                   rhs=stateb[:, g * 256:(g + 1) * 256], start=True, stop=True),
                     reads=[K("bcT"), K("stateb")], writes=[PS(5)])
            P.op("dve", lambda e: e.tensor_tensor(out=t1.rearrange("p (h q) -> p h q", h=8),
                                                  in0=pbank(5).rearrange("p (h q) -> p h q", h=8),
                                                  in1=eacs.unsqueeze(2).broadcast_to([128, 8, 64]), op=OP.mult),
                 reads=[PS(5), K("dec")], writes=[K("t1")])
            P.op("dve", lambda e: e.tensor_tensor(out=t2, in0=pbank(7), in1=t1, op=OP.add),
                 reads=[PS(7), K("t1")], writes=[K("t2")])
            P.op("dve", lambda e: e.tensor_tensor(out=t2, in0=t2, in1=xsk, op=OP.add),
                 reads=[K("t2"), K("xsk")], writes=[K("t2")])
            P.op("dve", lambda e: e.tensor_tensor(out=t2, in0=t2, in1=zs, op=OP.mult),
                 reads=[K("t2"), K("zs")], writes=[K("t2")])
            ss2, rs2 = small[:, 4:6], small[:, 6:8]
            for g in range(2):
                P.op("act", lambda e, g=g: e.activation(out=junk[:, 0:256], in_=t2[:, g * 256:(g + 1) * 256],
                                                        func=AF.Square, accum_out=ss2[:, g:g + 1]),
                     reads=[K("t2")], writes=[K("junk"), K("ss2")])
            P.op("act", lambda e: e.activation(out=rs2, in_=ss2, func=AF.Sqrt, scale=1.0 / 256, bias=epsc),
                 reads=[K("ss2"), K("epsc")], writes=[K("rs2")])
            P.op("dve", lambda e: e.reciprocal(out=rs2, in_=rs2), reads=[K("rs2")], writes=[K("rs2")])
            for g in range(2):
                P.op("dve", lambda e, g=g: e.tensor_scalar(out=mix[:, 512 + g * 256:512 + (g + 1) * 256],
                                                           in0=t2[:, g * 256:(g + 1) * 256], scalar1=rs2[:, g:g + 1],
                                                           scalar2=None, op0=OP.mult),
                     reads=[K("t2"), K("rs2")], writes=[K("mixs")])
        P.stage(10 + slot + 0.5)
        for g in range(2):
            P.op("pe", lambda e, g=g: e.matmul(out=pbank(bct)[:, g * 256:(g + 1) * 256], lhsT=btok[:, g, :],
                                               rhs=xdec[:, g * 256:(g + 1) * 256], start=True, stop=True),
                 reads=[K("btok"), K("xdec")], writes=[PS(bct)])
        P.op("dve", lambda e: e.tensor_tensor(out=stmp.rearrange("p (h q) -> p h q", h=8),
                                               in0=state.rearrange("p (h q) -> p h q", h=8),
                                               in1=cdec.unsqueeze(2).broadcast_to([128, 8, 64]), op=OP.mult),
             reads=[K("state"), K("dec")], writes=[K("stmp")])
        P.op("dve", lambda e: e.tensor_tensor(out=state, in0=pbank(bct), in1=stmp, op=OP.add),
             reads=[PS(bct), K("stmp")], writes=[K("state")])
        P.op("act", lambda e: e.copy(out=stateb, in_=state), reads=[K("state")], writes=[K("stateb")])

        if own:
            P.capture = cap_att
            bsel = 0 if ti == 0 else 1
            pi = 0
            for kv in range(2):
                for kb in range(2):
                    b = 1 + (pi % 2)
                    kpar = ppar if kb == 0 else par
                    bj = bsel if kb == 0 else 2
                    P.op("pe", lambda e, b=b, bj=bj, kv=kv: e.matmul(out=pbank(b), lhsT=ident_b, rhs=abias[:, bj, kv, :],
                                                                     start=True, stop=False),
                         reads=[K("identb"), K("abias"), K("abias0"), K("abias1")], writes=[PS(b)])
                    for g in range(4):
                        hq = kv * 4 + g
                        c, hf = hq // 2, hq % 2
                        P.op("pe", lambda e, b=b, g=g, c=c, hf=hf, kv=kv, kpar=kpar: e.matmul(
                            out=pbank(b)[:, g * 128:(g + 1) * 128], lhsT=kT[kpar][:, kv, hf, :],
                            rhs=qT[:, c, :], start=False, stop=(g == 3)),
                            reads=[K("kT%d" % kpar), K("qT")], writes=[PS(b)])
                    P.op("act", lambda e, b=b, pi=pi: e.activation(out=pT_sb[pi], in_=pbank(b), func=AF.Exp, scale=0.125),
                         reads=[PS(b)], writes=[K("pT%d" % pi)])
                    pi += 1
            P.stage(10 + slot + 0.7)
            pav = [pbank(3)[:, 0:260].rearrange("p (h q) -> p h q", h=4), pbank(4)[:, 0:260].rearrange("p (h q) -> p h q", h=4)]
            for hq in range(8):
                kv, g = hq // 4, hq % 4
                for kb in range(2):
                    kpar = ppar if kb == 0 else par
                    P.op("pe", lambda e, hq=hq, kv=kv, g=g, kb=kb, kpar=kpar: e.matmul(
                        out=pav[kv][:, g, :], lhsT=pT_sb[kv * 2 + kb][:, g * 128:(g + 1) * 128], rhs=vext[kpar][:, kv, 0:65],
                        start=(kb == 0), stop=(kb == 1)),
                        reads=[K("pT%d" % (kv * 2 + kb)), K("vext%d" % kpar)], writes=[PS(3 + kv)])
            den, rden = dts[:, 32:40], dts[:, 40:48]
            for kv in range(2):
                P.op("dve", lambda e, kv=kv: e.tensor_tensor(out=den[:, kv * 4:(kv + 1) * 4], in0=pav[kv][:, :, 64],
                                                             in1=esink[:, kv * 4:(kv + 1) * 4], op=OP.add),
                     reads=[PS(3 + kv), K("esink")], writes=[K("den%d" % kv)])
            P.op("dve", lambda e: e.reciprocal(out=rden, in_=den), reads=[K("den0"), K("den1")], writes=[K("rden")])
            for kv in range(2):
                P.op("dve", lambda e, kv=kv: e.tensor_tensor(
                    out=attn[:, kv * 256:(kv + 1) * 256].rearrange("p (h q) -> p h q", h=4), in0=pav[kv][:, :, 0:64],
                    in1=rden[:, kv * 4:(kv + 1) * 4].unsqueeze(2).broadcast_to([128, 4, 64]), op=OP.mult),
                    reads=[PS(3 + kv), K("rden")], writes=[K("attn%d" % kv)])
            ss3, rs3 = small[:, 8:9], small[:, 9:10]
            P.op("act", lambda e: e.activation(out=junk[:, 0:512], in_=attn, func=AF.Square, accum_out=ss3),
                 reads=[K("attn0"), K("attn1")], writes=[K("junk"), K("ss3")])
            P.op("act", lambda e: e.activation(out=rs3, in_=ss3, func=AF.Sqrt, scale=1.0 / 512, bias=epsc),
                 reads=[K("ss3"), K("epsc")], writes=[K("rs3")])
            P.op("dve", lambda e: e.reciprocal(out=rs3, in_=rs3), reads=[K("rs3")], writes=[K("rs3")])
            P.op("dve", lambda e: e.tensor_scalar(out=mix[:, 0:512], in0=attn, scalar1=rs3, scalar2=None, op0=OP.mult),
                 reads=[K("attn0"), K("attn1"), K("rs3")], writes=[K("mixa")])
            P.capture = None
            P.merged(cap_ssd, cap_att)
            if dbg:
                P.dma("sp", lambda e: e.dma_start(out=dbg_mix[ti * 128:(ti + 1) * 128, :], in_=mix), reads=[K("mixa"), K("mixs")], writes=[("dram", "dbgm")], cls="dbgm")
            ptm = pbank(0, BF16)
            for k in range(8):
                P.op("pe", lambda e, k=k: e.transpose(out=ptm[:, k * 128:(k + 1) * 128], in_=mix[:, k * 128:(k + 1) * 128],
                                                      identity=ident_b),
                     reads=[K("mixa"), K("mixs"), K("identb")], writes=[PS(0)])
            P.op("act", lambda e: e.copy(out=mixT, in_=ptm.rearrange("p (k t) -> p k t", k=8)),
                 reads=[PS(0)], writes=[K("mixT")])
            for half in range(2):
                b = 5 + 2 * half
                for k in range(8):
                    P.op("pe", lambda e, half=half, b=b, k=k: e.matmul(
                        out=pbank(b), lhsT=mixT[:, k, :], rhs=wo_sb[:, k, half * 512:(half + 1) * 512],
                        start=(k == 0), stop=(k == 7)),
                        reads=[K("mixT"), WOKEYS[k]], writes=[PS(b)])
                P.op("dve", lambda e, half=half, b=b: e.tensor_tensor(
                    out=h[:, ti, half * 512:(half + 1) * 512], in0=pbank(b), in1=h[:, ti, half * 512:(half + 1) * 512],
                    op=OP.add),
                    reads=[PS(b), srckey], writes=[srckey])

    npipe = max(npre - 1, 0)
    if npipe > 0:
        def cap(fn):
            lst = []
            P.capture = lst
            fn()
            P.capture = None
            return lst

        b2_pending = {}
        for k in range(npipe + 2):
            fl = cap(lambda: mixer_tile(k, "front")) if k < npipe else []
            b1 = []
            if 0 <= k - 1 < npipe:
                full = cap(lambda: mixer_tile(k - 1, "back"))
                mi = [i for i, it in enumerate(full) if it[0] == "marker"]
                assert len(mi) == 1
                b1 = full[:mi[0]]
                b2_pending[k - 1] = full[mi[0] + 1:]
            b2 = b2_pending.pop(k - 2, [])
            P.merged(fl, b1, b2, speed=[1.0, 1.0, 0.7])
    for slot in range(npipe, npre + nt):
        P.stage(10 + slot)
        mixer_tile(slot)
    P.stage(100)

    if dbg:
        for ti in range(nt):
            P.dma("sp", lambda e, ti=ti: e.dma_start(out=dbg_h[ti * 128:(ti + 1) * 128, :], in_=h[:, ti, :]),
                  reads=[K("h%d" % ti)], writes=[("dram", "dbg%d" % ti)], force=True, cls="dbg")

    P.barrier()
    P.stage(101)
    off[0] = persist_end
    xn2T = alloc([8, ntok], BF16)
    qTp = alloc([8, ntok], BF16)
    ntau = alloc([nt, 8], F32)
    nL = alloc([nt, 8], F32)
    fng = alloc([1024], F32)
    k12 = alloc([2, 128], BF16)
    p2_base = off[0]
    wq_sb = alloc([8, 1024], BF16)
    sc = alloc([8, 2, 128], F32)
    work16 = alloc([16, 128], F32)
    work8 = alloc([8, 256], F32)
    m16 = alloc([16, 16], F32)
    cand = alloc([8, 256], F32)
    c16 = alloc([8, 16], F32)
    d16 = alloc([8, 16], F32)
    zsum = alloc([8], F32)
    junk2 = alloc([1024], BF16)
    xs2 = alloc([1024], BF16)
    p2a_end = off[0]

    P.dma("sp", lambda e: e.dma_start(out=fng, in_=fng_d), writes=[K("fng")], cls="c0")
    P.dma("pool", lambda e: e.dma_start(out=k12, in_=k12_d), writes=[K("k12")], cls="c1")
    P.dma("pool", lambda e: e.dma_start(out=wq_sb, in_=wq_d), writes=[K("wq%d" % k) for k in range(8)], cls="c2")
    WQK = [K("wq%d" % k) for k in range(8)]

    for ti in range(nt):
        ss, rs = small[:, 0:1], small[:, 1:2]
        src, srckey = h[:, ti, :], K("h%d" % ti)
        P.op("act", lambda e, src=src: e.activation(out=junk2, in_=src, func=AF.Square, accum_out=ss),
             reads=[srckey], writes=[K("junk2"), K("ss")])
        P.op("act", lambda e: e.activation(out=rs, in_=ss, func=AF.Sqrt, scale=1.0 / 1024, bias=epsc),
             reads=[K("ss"), K("epsc")], writes=[K("rs")])
        P.op("dve", lambda e: e.reciprocal(out=rs, in_=rs), reads=[K("rs")], writes=[K("rs")])
        P.op("dve", lambda e, src=src: e.tensor_scalar(out=xs2, in0=src, scalar1=rs, scalar2=None, op0=OP.mult),
             reads=[srckey, K("rs")], writes=[K("xs2")])
        pt = pbank(0, BF16)
        for k in range(8):
            P.op("pe", lambda e, k=k: e.transpose(out=pt[:, k * 128:(k + 1) * 128], in_=xs2[:, k * 128:(k + 1) * 128],
                                                  identity=ident_b),
                 reads=[K("xs2"), K("identb")], writes=[PS(0)])
        P.op("dve", lambda e, ti=ti: e.tensor_tensor(out=xn2T[:, :, ti * 128:(ti + 1) * 128],
                                                     in0=pt.rearrange("p (k t) -> p k t", k=8),
                                                     in1=g_ffn.unsqueeze(2).broadcast_to([128, 8, 128]), op=OP.mult),
             reads=[PS(0), K("cpart")], writes=[K("xn2T%d" % ti)])
    XNK = [K("xn2T%d" % ti) for ti in range(nt)]
    ngrp = (ntok + 511) // 512
    for tg in range(ngrp):
        t0 = tg * 512
        tn = min(512, ntok - t0)
        for hh in range(8):
            b = 1 + (hh % 2)
            for k in range(8):
                P.op("pe", lambda e, hh=hh, k=k, b=b, t0=t0, tn=tn: e.matmul(
                    out=pbank(b)[:, 0:tn], lhsT=wq_sb[:, k, hh * 128:(hh + 1) * 128], rhs=xn2T[:, k, t0:t0 + tn],
                    start=(k == 0), stop=(k == 7)),
                    reads=[WQK[k]] + XNK[tg * 4:tg * 4 + 4], writes=[PS(b)])
            P.op("act", lambda e, hh=hh, b=b, t0=t0, tn=tn: e.copy(out=qTp[:, hh, t0:t0 + tn], in_=pbank(b)[:, 0:tn]),
                 reads=[PS(b)], writes=[K("qTp%d" % tg)])

    def scores(ti, banks):
        tg = ti // 4
        for hh in range(8):
            b = banks[hh // 2]
            for hf in range(2):
                o = ((hh % 2) * 2 + hf) * 128
                P.op("pe", lambda e, hh=hh, hf=hf, b=b, o=o: e.matmul(
                    out=pbank(b)[:, o:o + 128], lhsT=qTp[:, hh, ti * 128:(ti + 1) * 128],
                    rhs=k12[:, hf, :], start=True, stop=True),
                    reads=[K("qTp%d" % tg), K("k12")], writes=[PS(b)])

    for ti in range(nt):
        scores(ti, [3, 4, 5, 6])
        for j in range(4):
            P.op("act", lambda e, j=j: e.copy(out=sc[:, 2 * j:2 * j + 2, :, :],
                                              in_=pbank(3 + j).rearrange("p (a b c) -> p a b c", a=2, b=2)),
                 reads=[PS(3 + j)], writes=[K("sc")])
        for i in range(16):
            hh, hf = i // 2, i % 2
            P.op("dve", lambda e, hh=hh, hf=hf, i=i: e.max(out=m16[:, i, 0:8], in_=sc[:, hh, hf, :]),
                 reads=[K("sc")], writes=[K("m16a%d" % i)])
        for i in range(16):
            hh, hf = i // 2, i % 2
            P.op("dve", lambda e, hh=hh, hf=hf, i=i: e.match_replace(out=work16[:, i, :], in_to_replace=m16[:, i, 0:8],
                                                                    in_values=sc[:, hh, hf, :], imm_value=-1e30),
                 reads=[K("sc"), K("m16a%d" % i)], writes=[K("wk%d" % i)])
        for i in range(16):
            P.op("dve", lambda e, i=i: e.max(out=m16[:, i, 8:16], in_=work16[:, i, :]),
                 reads=[K("wk%d" % i)], writes=[K("m16b%d" % i)])
        m4 = m16.rearrange("p (h f) k -> p h f k", f=2)
        P.op("dve", lambda e: e.tensor_tensor(out=cand.rearrange("p h (a b) -> p h a b", a=16),
                                              in0=m4[:, :, 0, :].unsqueeze(3).broadcast_to([128, 8, 16, 16]),
                                              in1=m4[:, :, 1, :].unsqueeze(2).broadcast_to([128, 8, 16, 16]), op=OP.add),
             reads=[K("m16a%d" % i) for i in range(16)] + [K("m16b%d" % i) for i in range(16)], writes=[K("cand")])
        for hh in range(8):
            P.op("dve", lambda e, hh=hh: e.max(out=c16[:, hh, 0:8], in_=cand[:, hh, :]),
                 reads=[K("cand")], writes=[K("c16a%d" % hh)])
        for hh in range(8):
            P.op("dve", lambda e, hh=hh: e.match_replace(out=work8[:, hh, :], in_to_replace=c16[:, hh, 0:8],
                                                         in_values=cand[:, hh, :], imm_value=-1e30),
                 reads=[K("cand"), K("c16a%d" % hh)], writes=[K("wc%d" % hh)])
        for hh in range(8):
            P.op("dve", lambda e, hh=hh: e.max(out=c16[:, hh, 8:16], in_=work8[:, hh, :]), reads=[K("wc%d" % hh)],
                 writes=[K("c16b%d" % hh)])
        C16A = [K("c16a%d" % hh) for hh in range(8)]
        C16B = [K("c16b%d" % hh) for hh in range(8)]
        P.op("dve", lambda e, ti=ti: e.tensor_scalar(out=ntau[:, ti, :], in0=c16[:, :, 15], scalar1=-1e-4, scalar2=None,
                                                     op0=OP.add),
             reads=C16B, writes=[K("ntau")])
        P.op("dve", lambda e: e.tensor_tensor(out=d16, in0=c16, in1=c16[:, :, 0:1].broadcast_to([128, 8, 16]),
                                              op=OP.subtract),
             reads=C16A + C16B, writes=[K("d16")])
        P.op("act", lambda e: e.activation(out=d16, in_=d16, func=AF.Exp), reads=[K("d16")], writes=[K("d16")])
        P.op("dve", lambda e: e.tensor_reduce(out=zsum, in_=d16, axis=AX.X, op=OP.add), reads=[K("d16")], writes=[K("zsum")])
        P.op("act", lambda e: e.activation(out=zsum, in_=zsum, func=AF.Ln), reads=[K("zsum")], writes=[K("zsum")])
        P.op("dve", lambda e, ti=ti: e.scalar_tensor_tensor(out=nL[:, ti, :], in0=zsum, scalar=-1.0, in1=c16[:, :, 0],
                                                            op0=OP.mult, op1=OP.subtract),
             reads=[K("zsum")] + C16A, writes=[K("nL")])

    P.stage(102)
    P.barrier()
    off[0] = p2_base
    ut = [alloc([8, 512], BF16) for _ in range(2)]
    vv = [alloc([4, 1024], BF16) for _ in range(2)]
    kk = [alloc([512], BF16) for _ in range(2)]
    Eb = [alloc([2, 512], BF16) for _ in range(2)]
    Wb = [alloc([8, 512], BF16) for _ in range(2)]
    Gb = alloc([512], BF16)
    gl = alloc([512], BF16)
    actbs = [alloc([512], BF16) for _ in range(3)]
    actTs = [alloc([4, 128], BF16) for _ in range(2)]

    def load_eg(eg):
        ub = eg % 2
        P.dma("pool", lambda e: e.dma_start(out=kk[ub], in_=kk_d[eg]), writes=[K("kk%d" % ub)], cls="k%d" % ub)
        P.dma("pool", lambda e: e.dma_start(out=ut[ub], in_=ut_d[eg]), writes=[K("ut%d" % ub)], cls="u%d" % ub)
        P.dma("pool", lambda e: e.dma_start(out=vv[ub], in_=v_d[eg]), writes=[K("vv%d" % ub)], cls="v%d" % ub)

    def headpair(eg, ti, n, hp):
        ub = eg % 2
        KK = K("kk%d" % ub)
        tg = ti // 4
        W = Wb[n % 2]
        wk = lambda i: K("W%d_%d" % (n % 2, i))
        banks = (3, 4) if hp % 2 == 0 else (5, 6)
        E = Eb[hp % 2]
        for hl in range(2):
            hh = hp * 2 + hl
            bnk = banks[hl]
            ek = K("E%d_%d" % (hp % 2, hl))
            P.op("pe", lambda e, hh=hh, bnk=bnk: e.matmul(out=pbank(bnk), lhsT=qTp[:, hh, ti * 128:(ti + 1) * 128],
                                                          rhs=kk[ub], start=True, stop=True),
                 reads=[K("qTp%d" % tg), KK], writes=[PS(bnk)])
        for hl in range(2):
            hh = hp * 2 + hl
            bnk = banks[hl]
            ek = K("E%d_%d" % (hp % 2, hl))
            P.op("act", lambda e, hh=hh, hl=hl, bnk=bnk, E=E: e.activation(out=E[:, hl, :], in_=pbank(bnk), func=AF.Exp,
                                                                          bias=nL[:, ti, hh:hh + 1]),
                 reads=[PS(bnk), K("nL")], writes=[ek])
            P.op("dve", lambda e, hh=hh, hl=hl, bnk=bnk, E=E: e.scalar_tensor_tensor(
                out=W[:, hh, :], in0=pbank(bnk), scalar=ntau[:, ti, hh:hh + 1], in1=E[:, hl, :],
                op0=OP.is_ge, op1=OP.mult),
                reads=[PS(bnk), K("ntau"), ek], writes=[wk(hh)])

    def amat(eg, ti, n):
        ub = eg % 2
        UK = K("ut%d" % ub)
        W = Wb[n % 2]
        wk = lambda i: K("W%d_%d" % (n % 2, i))
        for k in range(8):
            P.op("pe", lambda e, k=k: e.matmul(out=pbank(1), lhsT=xn2T[:, k, ti * 128:(ti + 1) * 128],
                                               rhs=ut[ub][:, k, :], start=(k == 0), stop=(k == 7)),
                 reads=[XNK[ti], UK], writes=[PS(1)])
        P.op("pool", lambda e: e.tensor_tensor(out=W[:, 0:2, :], in0=W[:, 0:2, :], in1=W[:, 2:4, :], op=OP.add),
             reads=[wk(0), wk(1), wk(2), wk(3)], writes=[wk(0), wk(1)])

    def tree(eg, ti, n):
        actb = actbs[n % 3]
        AK = K("actb%d" % (n % 3))
        W = Wb[n % 2]
        wk = lambda i: K("W%d_%d" % (n % 2, i))
        P.op("act", lambda e: e.activation(out=gl, in_=pbank(1), func=AF.Gelu), reads=[PS(1)], writes=[K("gl")])
        P.op("dve", lambda e: e.tensor_tensor(out=W[:, 4:6, :], in0=W[:, 4:6, :], in1=W[:, 6:8, :], op=OP.add),
             reads=[wk(4), wk(5), wk(6), wk(7)], writes=[wk(4), wk(5)])
        P.op("dve", lambda e: e.tensor_tensor(out=W[:, 0:2, :], in0=W[:, 0:2, :], in1=W[:, 4:6, :], op=OP.add),
             reads=[wk(0), wk(1), wk(4), wk(5)], writes=[wk(0), wk(1)])
        P.op("dve", lambda e: e.tensor_tensor(out=Gb, in0=W[:, 0, :], in1=W[:, 1, :], op=OP.add),
             reads=[wk(0), wk(1)], writes=[K("Gb")])
        P.op("dve", lambda e: e.tensor_tensor(out=actb, in0=gl, in1=Gb, op=OP.mult),
             reads=[K("gl"), K("Gb")], writes=[AK])

    def tpose(eg, ti, n):
        actb = actbs[n % 3]
        actT = actTs[n % 2]
        AK = K("actb%d" % (n % 3))
        TK = K("actT%d" % (n % 2))
        ptb = pbank(2, BF16)
        for et in range(4):
            P.op("pe", lambda e, et=et: e.transpose(out=ptb[:, et * 128:(et + 1) * 128],
                                                    in_=actb[:, et * 128:(et + 1) * 128], identity=ident_b),
                 reads=[AK, K("identb")], writes=[PS(2)])
        P.op("act", lambda e: e.copy(out=actT, in_=ptb[:, 0:512].rearrange("p (a t) -> p a t", a=4)),
             reads=[PS(2)], writes=[TK])

    def vmat_pe(eg, ti, n):
        ub = eg % 2
        VK = K("vv%d" % ub)
        actT = actTs[n % 2]
        TK = K("actT%d" % (n % 2))
        for et in range(4):
            for half in range(2):
                bnk = 7 if half == 0 else 0
                P.op("pe", lambda e, half=half, bnk=bnk, et=et: e.matmul(
                    out=pbank(bnk), lhsT=actT[:, et, :], rhs=vv[ub][:, et, half * 512:(half + 1) * 512],
                    start=(et == 0), stop=(et == 3)),
                    reads=[TK, VK], writes=[PS(bnk)])

    def hadd(eg, ti, n, half):
        bnk = 7 if half == 0 else 0
        P.op("dve", lambda e: e.tensor_tensor(
            out=h[:, ti, half * 512:(half + 1) * 512], in0=pbank(bnk), in1=h[:, ti, half * 512:(half + 1) * 512],
            op=OP.add),
            reads=[PS(bnk), K("h%d" % ti)], writes=[K("h%d" % ti)])

    load_eg(0)
    if neg > 1:
        load_eg(1)
    steps = [(eg, ti) for eg in range(neg) for ti in range(nt)]
    ns = len(steps)
    for n in range(-1, ns + 1):
        nxt = steps[n + 1] + (n + 1,) if n + 1 < ns else None
        cur = steps[n - 1] + (n - 1,) if 0 <= n - 1 < ns else None
        if nxt:
            headpair(*nxt, 0)
            headpair(*nxt, 1)
        if cur:
            tpose(*cur)
        if nxt:
            amat(*nxt)
            headpair(*nxt, 2)
            headpair(*nxt, 3)
        if cur:
            vmat_pe(*cur)
        if nxt:
            tree(*nxt)
        if cur:
            hadd(*cur, 0)
            hadd(*cur, 1)
        if cur:
            eg, ti = cur[0], cur[1]
            if ti == nt - 1 and eg + 2 < neg:
                load_eg(eg + 2)

    P.stage(103)
    P.barrier()
    off[0] = p2_base
    osbs = [alloc([1024], F32) for _ in range(2)]
    for ti in range(nt):
        osb = osbs[ti % 2]
        ok = K("osb%d" % (ti % 2))
        ss, rs = small[:, 0:1], small[:, 1:2]
        src, srckey = h[:, ti, :], K("h%d" % ti)
        P.op("act", lambda e, src=src, osb=osb: e.activation(out=osb, in_=src, func=AF.Square, accum_out=ss),
             reads=[srckey], writes=[ok, K("ss")])
        P.op("act", lambda e: e.activation(out=rs, in_=ss, func=AF.Sqrt, scale=1.0 / 1024, bias=epsc),
             reads=[K("ss"), K("epsc")], writes=[K("rs")])
        P.op("dve", lambda e: e.reciprocal(out=rs, in_=rs), reads=[K("rs")], writes=[K("rs")])
        P.op("dve", lambda e, src=src, osb=osb: e.scalar_tensor_tensor(out=osb, in0=src, scalar=rs, in1=fng,
                                                                      op0=OP.mult, op1=OP.mult),
             reads=[srckey, K("rs"), K("fng"), ok], writes=[ok])
        P.dma("sp", lambda e, ti=ti, osb=osb: e.dma_start(out=out_d[ti * 128:(ti + 1) * 128, :], in_=osb),
              reads=[ok], writes=[("dram", "out%d" % ti)], cls="o%d" % (ti % 2))
    P.op("sp", None, reads=[("dram", "out%d" % ti) for ti in range(nt)] +
         ([("dram", "dbg%d" % ti) for ti in range(nt)] if dbg else []), writes=[("fin",)], force=True)
    P.op("act", None, reads=[("fin",)], writes=[], force=True)

    sems = {e: enter(nc.semaphore("s_" + e)) for e in Prog.ENG}
    dsems = {c: enter(nc.semaphore("d_" + c)) for c in sorted(P.dcount.keys())}
    block = enter(nc.Block())
    P.emit(nc, sems, dsems, block)
    for cm in reversed(ctx):
        cm.__exit__(None, None, None)
    return nc


def _consts():
    ident = np.eye(128, dtype=np.float32)
    idx = np.arange(128)
    tri_le = (idx[:, None] <= idx[None, :]).astype(np.float32)
    tri_gt = (idx[:, None] > idx[None, :]).astype(np.float32)
    ones = np.ones((128, 128), np.float32)
    cst = np.stack([ident, tri_le, tri_gt, ones], axis=1)
    slopes = np.exp2(-(8.0 / 8) * np.arange(1, 9)).astype(np.float32)
    s = idx[:, None]
    t = idx[None, :]
    bias = np.full((3, 128, 2, 4, 128), NEGBIG, np.float32)
    for kv in range(2):
        for g in range(4):
            sl = slopes[kv * 4 + g] * 8.0
            dprev = 128 + t - s
            bp = np.where(dprev < 128, -sl * dprev, NEGBIG)
            dcur = t - s
            bc = np.where(dcur >= 0, -sl * dcur, NEGBIG)
            bias[1, :, kv, g, :] = bp
            bias[2, :, kv, g, :] = bc
    return np.ascontiguousarray(cst), slopes, bias.reshape(3, 128, 2, 512)


def prep_inputs(x, norm_mix_g, w_in, attn_sinks, attn_out_g, conv_w, conv_b, dt_bias, a_log, d_skip,
                ssm_norm_g, w_out, norm_ffn_g, peer_wq, peer_sub_keys, peer_u, peer_v, final_norm_g,
                nt=NT, npre=NPRE, neg=NEG):
    f = np.float32
    x = np.asarray(x, f)
    W = np.asarray(w_in[0], f)
    q, k, v, z, xbc, dt = W[:, 0:512], W[:, 512:640], W[:, 640:768], W[:, 768:1280], W[:, 1280:2304], W[:, 2304:2312]
    zz = np.zeros((1024, 64), f)
    wcat = np.concatenate([q, k[:, 0:64], zz, zz, k[:, 0:64], k[:, 64:128], zz, zz, k[:, 64:128], xbc, z, v, dt], axis=1)
    assert wcat.shape[1] == WF + WT
    lay = lambda m: np.ascontiguousarray(m.reshape(8, 128, m.shape[1]).transpose(1, 0, 2))
    col = lambda vec: np.ascontiguousarray(np.asarray(vec, f).reshape(8, 128).T)
    rep = lambda vec: np.broadcast_to(np.asarray(vec, f)[None, :], (128, len(vec)))
    cw = np.asarray(conv_w[0], f)
    cpart = np.concatenate([col(norm_mix_g[0]), col(norm_ffn_g[0]),
                            col(np.concatenate([attn_out_g[0], ssm_norm_g[0]])),
                            np.concatenate([col(cw[i]) for i in range(4)], axis=1), col(conv_b[0]),
                            np.zeros((128, 8), f)], axis=1)
    cvec = np.concatenate([rep(attn_sinks[0]), rep(dt_bias[0]), rep(a_log[0]), rep(d_skip[0])], axis=1)
    cst, _, abias = _consts()
    sk = np.asarray(peer_sub_keys[0], f)
    k12 = np.zeros((128, 2, 128), f)
    k12[0:64, 0, :] = sk[0].T
    k12[64:128, 1, :] = sk[1].T
    kkh = np.empty((NEG, 128, 4, 128), f)
    kkh[:, 0:64] = sk[0].reshape(NEG, 4, 64).transpose(0, 2, 1)[:, :, :, None]
    kkh[:, 64:128] = sk[1].T[None, :, None, :]
    kkh = np.ascontiguousarray(kkh.reshape(NEG, 128, 512))
    U = np.asarray(peer_u[0], f)
    ut = np.ascontiguousarray(U.T.reshape(8, 128, NEG, 512).transpose(2, 1, 0, 3))
    V = np.asarray(peer_v[0], f)
    vv = np.ascontiguousarray(V.reshape(NEG, 4, 128, 1024).transpose(0, 2, 1, 3))
    shared = dict(w_in=lay(wcat), w_out=lay(np.asarray(w_out[0], f)), wq=lay(np.asarray(peer_wq[0], f)),
                  k12=np.ascontiguousarray(k12), ut=ut[:neg], vv=vv[:neg], kk=kkh[:neg], cvec=np.ascontiguousarray(cvec),
                  cpart=np.ascontiguousarray(cpart), fng=np.ascontiguousarray(rep(final_norm_g)), cst=cst)
    in_maps = []
    tiles_per_seq = x.shape[1] // 128
    segs = tiles_per_seq // nt
    for c in range(NCORES):
        b, s = c // segs, c % segs
        t0 = s * nt
        xo = x[b, t0 * 128:(t0 + nt) * 128]
        xp = np.zeros((max(npre, 1) * 128, 1024), f)
        pf = np.zeros((128, max(npre, 1)), f)
        for j in range(npre):
            gt = t0 - npre + j
            if gt >= 0:
                xp[j * 128:(j + 1) * 128] = x[b, gt * 128:(gt + 1) * 128]
                pf[:, j] = 1.0
        ab = abias.copy()
        ab[0] = abias[1] if s > 0 else NEGBIG
        m = dict(shared)
        m.update(x_own=np.ascontiguousarray(xo), x_pre=xp, pflag=pf, abias=np.ascontiguousarray(ab))
        in_maps.append(m)
    return in_maps


_NC_CACHE = {}


def kernel(**inputs):
    in_maps = prep_inputs(**inputs)
    if "nc" not in _NC_CACHE:
        _NC_CACHE["nc"] = build()
    nc = _NC_CACHE["nc"]
    res = run_bass_kernel_spmd(nc, in_maps, core_ids=list(range(NCORES)))
    outs = [np.asarray(r["out"], np.float32) for r in res.results]
    x = inputs["x"]
    return np.concatenate(outs, axis=0).reshape(x.shape).astype(np.float32)
```
